# Optimizing a Trainium2 kernel written in Bass

```python
import jax, jax.numpy as jnp
from jax import lax
import numpy as np

D_MODEL = 1024
BATCH = 8
SEQ = 4096
DEPTH = 1

CONV_WIDTH = D_MODEL
CONV_KERNEL = 31
HEAD_DIM = 128
HEADS_PER_GROUP = D_MODEL // 256
DILATED_GROUPS = ((128, 1), (512, 4), (2048, 16))
N_GROUPS = len(DILATED_GROUPS)
ATTN_HEADS = HEADS_PER_GROUP * N_GROUPS
QKV_WIDTH = ATTN_HEADS * HEAD_DIM
ATTN_WIDTH = HEADS_PER_GROUP * HEAD_DIM
BLK = 128
ROPE_THETA = 10000.0
EPS = 1e-6
NEG_INF = -1e30

SPLIT_SIZES = (CONV_WIDTH, CONV_WIDTH, CONV_WIDTH,
               QKV_WIDTH, QKV_WIDTH, QKV_WIDTH,
               ATTN_WIDTH,
               D_MODEL, D_MODEL)
D_IN = sum(SPLIT_SIZES)
SPLIT_POINTS = tuple(int(v) for v in np.cumsum(SPLIT_SIZES)[:-1])

kernel_name = "hybrid_conformer_dilated_attn_block"


def rms_norm(x, w):
    xf = x.astype(jnp.float32)
    return xf * lax.rsqrt(jnp.mean(xf * xf, axis=-1, keepdims=True) + EPS) * w.astype(jnp.float32)


def layer_norm(x, w, b):
    xf = x.astype(jnp.float32)
    mu = jnp.mean(xf, axis=-1, keepdims=True)
    var = jnp.mean(jnp.square(xf - mu), axis=-1, keepdims=True)
    return (xf - mu) * lax.rsqrt(var + EPS) * w.astype(jnp.float32) + b.astype(jnp.float32)


def rope(x, positions):
    inv_freq = ROPE_THETA ** (-jnp.arange(0, HEAD_DIM, 2, dtype=jnp.float32) / HEAD_DIM)
    ang = positions.astype(jnp.float32)[..., None] * inv_freq
    cos = jnp.cos(ang)[:, :, None, :]
    sin = jnp.sin(ang)[:, :, None, :]
    xf = x.astype(jnp.float32)
    x1, x2 = xf[..., : HEAD_DIM // 2], xf[..., HEAD_DIM // 2:]
    return jnp.concatenate([x1 * cos - x2 * sin, x2 * cos + x1 * sin], axis=-1).astype(x.dtype)


def depthwise_causal_conv(u, w, b):
    out = lax.conv_general_dilated(
        u, w.astype(u.dtype)[:, None, :], window_strides=(1,),
        padding=[(CONV_KERNEL - 1, 0)], dimension_numbers=("NWC", "WIO", "NWC"),
        feature_group_count=u.shape[-1])
    return out + b.astype(u.dtype)


def dilated_window_attention(q, k, v, window, dilation):
    bsz, seq, nh, hd = q.shape
    steps = window // dilation
    span = dilation * BLK
    seq_pad = -(-seq // span) * span
    nb = seq_pad // span
    pad = ((0, 0), (0, seq_pad - seq), (0, 0), (0, 0))

    def blocks(t):
        return jnp.pad(t, pad).reshape(bsz, nb, BLK, dilation, nh, hd)

    def with_prev(t):
        prev = jnp.pad(t[:, :-1], ((0, 0), (1, 0), (0, 0), (0, 0), (0, 0), (0, 0)))
        return jnp.concatenate([prev, t], axis=2)

    qb = blocks(q)
    kk = with_prev(blocks(k))
    vv = with_prev(blocks(v))
    s = jnp.einsum("bnqrhd,bnkrhd->bnrhqk", qb, kk).astype(jnp.float32) * (hd ** -0.5)
    qi = jnp.arange(BLK)[:, None]
    kj = jnp.arange(2 * BLK)[None, :]
    diff = BLK + qi - kj
    band = (diff >= 0) & (diff <= steps)
    valid_prev = (jnp.arange(nb)[:, None, None] > 0) | (kj[None] >= BLK)
    mask = band[None] & valid_prev
    s = jnp.where(mask[:, None, None], s, NEG_INF)
    lse = jax.nn.logsumexp(s, axis=-1)
    p = jnp.exp(s - lse[..., None])
    o = jnp.einsum("bnrhqk,bnkrhd->bnqrhd", p.astype(v.dtype), vv)
    o = o.reshape(bsz, seq_pad, nh, hd)[:, :seq]
    lse = lse.transpose(0, 1, 4, 2, 3).reshape(bsz, seq_pad, nh)[:, :seq]
    return o, lse


def setup_inputs(seed: int = 0) -> dict:
    key = jax.random.key(seed)
    ks = jax.random.split(key, 17)
    f32 = jnp.float32
    nrm = lambda k, shape, s: jax.random.normal(k, shape, f32) * s
    return {
        "x": nrm(ks[0], (BATCH, SEQ, D_MODEL), 1.0),
        "c": nrm(ks[1], (BATCH, D_MODEL), 1.0),
        "positions": jnp.broadcast_to(jnp.arange(SEQ, dtype=jnp.int32), (BATCH, SEQ)),
        "norm_w": 1.0 + nrm(ks[2], (D_MODEL,), 0.02),
        "w_ada": nrm(ks[3], (D_MODEL, 3 * D_MODEL), 0.5 * D_MODEL ** -0.5),
        "b_ada": nrm(ks[4], (3 * D_MODEL,), 0.02),
        "w_in": nrm(ks[5], (D_MODEL, D_IN), D_MODEL ** -0.5),
        "conv_w": nrm(ks[6], (CONV_KERNEL, CONV_WIDTH), CONV_KERNEL ** -0.5),
        "conv_b": nrm(ks[7], (CONV_WIDTH,), 0.02),
        "conv_ln_w": 1.0 + nrm(ks[8], (CONV_WIDTH,), 0.02),
        "conv_ln_b": nrm(ks[9], (CONV_WIDTH,), 0.02),
        "w_conv_out": nrm(ks[10], (CONV_WIDTH, D_MODEL), CONV_WIDTH ** -0.5),
        "q_norm_w": 1.0 + nrm(ks[11], (HEAD_DIM,), 0.02),
        "k_norm_w": 1.0 + nrm(ks[12], (HEAD_DIM,), 0.02),
        "w_attn_out": nrm(ks[13], (ATTN_WIDTH, D_MODEL), ATTN_WIDTH ** -0.5),
        "w_out": nrm(ks[14], (D_MODEL, D_MODEL), D_MODEL ** -0.5),
    }


def reference(x, c, positions, norm_w, w_ada, b_ada, w_in, conv_w, conv_b, conv_ln_w,
              conv_ln_b, w_conv_out, q_norm_w, k_norm_w, w_attn_out, w_out):
    dt = x.dtype
    bsz, seq, _ = x.shape
    for _layer in range(DEPTH):
        mod = jax.nn.silu(c) @ w_ada + b_ada
        shift, scale, gate = jnp.split(mod, 3, axis=-1)
        h = (rms_norm(x, norm_w) * (1.0 + scale[:, None].astype(jnp.float32))
             + shift[:, None].astype(jnp.float32)).astype(dt)

        z = h @ w_in
        a, b, g_conv, q, k, v, g_attn, m_conv, m_attn = jnp.split(z, SPLIT_POINTS, axis=-1)

        u = a * jax.nn.sigmoid(b)
        u = depthwise_causal_conv(u, conv_w, conv_b)
        u = jax.nn.silu(layer_norm(u, conv_ln_w, conv_ln_b)).astype(dt)
        u = u * jax.nn.silu(g_conv)
        y_conv = u @ w_conv_out

        q = q.reshape(bsz, seq, ATTN_HEADS, HEAD_DIM)
        k = k.reshape(bsz, seq, ATTN_HEADS, HEAD_DIM)
        v = v.reshape(bsz, seq, ATTN_HEADS, HEAD_DIM)
        q = rope(rms_norm(q, q_norm_w).astype(dt), positions)
        k = rope(rms_norm(k, k_norm_w).astype(dt), positions)
        outs, lses = [], []
        for g, (window, dilation) in enumerate(DILATED_GROUPS):
            sl = slice(g * HEADS_PER_GROUP, (g + 1) * HEADS_PER_GROUP)
            o_g, l_g = dilated_window_attention(q[:, :, sl], k[:, :, sl], v[:, :, sl], window, dilation)
            outs.append(o_g)
            lses.append(l_g)
        o = jnp.stack(outs, axis=0)
        wts = jax.nn.softmax(jnp.stack(lses, axis=0), axis=0)
        o = jnp.sum(wts[..., None] * o.astype(jnp.float32), axis=0).astype(dt)
        o = o.reshape(bsz, seq, ATTN_WIDTH) * jax.nn.silu(g_attn)
        y_attn = o @ w_attn_out

        y = jax.nn.sigmoid(m_conv) * y_conv + jax.nn.sigmoid(m_attn) * y_attn
        out = y @ w_out
        x = (x + gate[:, None] * out).astype(dt)
    return x
```

```python
import numpy as np
from contextlib import ExitStack
import concourse.bass as bass
import concourse.mybir as mybir
from concourse.bass_utils import run_bass_kernel_spmd

F32 = mybir.dt.float32
BF16 = mybir.dt.bfloat16
I32 = mybir.dt.int32
AF = mybir.ActivationFunctionType
ALU = mybir.AluOpType

SEQ = 4096
D = 1024
DIN = 10240
NCORE = 8
EPS = 1e-6
GROUPS = ((128, 1), (512, 4), (2048, 16))
SAME_ENGINE_SYNC = True


class Buf:
    def __init__(s, name, excl=False):
        s.name = name
        s.nobar = False
        s.excl = excl
        s.w = []
        s.r = []
        s.gd = []
        s.dsem = None
        s.dcnt = 0

    def newgen(s):
        s.gd = s.w + s.r
        s.w = []
        s.r = []


class Eng:
    def __init__(s, name, sem):
        s.name = name
        s.sem = sem
        s.count = 0
        s.ops = []
        s.waited = {}


class Sched:
    def __init__(s, nc, stack):
        s.nc = nc
        s.stack = stack
        s.engs = {}
        for n in ["pe", "act", "dve", "pool", "sp"]:
            s.engs[n] = Eng(n, stack.enter_context(nc.semaphore("es_" + n)))
        s.dbufs = []

    def _waits(s, e, deps):
        need = {}
        for (sem, val) in deps:
            if sem is e.sem and not SAME_ENGINE_SYNC:
                continue
            k = sem.num
            if k not in need or val > need[k][1]:
                need[k] = (sem, val)
        out = []
        for k, (sem, val) in need.items():
            if e.waited.get(k, 0) >= val:
                continue
            e.waited[k] = val
            out.append((sem, val))
        return out

    @staticmethod
    def _deps(e, reads, writes, pwrites):
        deps = []
        for b in reads:
            deps += b.w
            if b.excl:
                deps += [t for t in b.r if t[0] is not e.sem]
        for b in writes:
            deps += b.gd + b.w + b.r
        for b in pwrites:
            deps += b.gd + b.r
        return deps

    @staticmethod
    def _record(tok, reads, writes, pwrites):
        for b in reads:
            b.r.append(tok)
        for b in writes:
            b.gd = b.w + b.r
            b.w = [tok]
            b.r = []
        for b in pwrites:
            b.w.append(tok)

    def op(s, eng, fn, reads=(), writes=(), pwrites=()):
        e = s.engs[eng]
        waits = s._waits(e, s._deps(e, reads, writes, pwrites))
        e.count += 1
        tok = (e.sem, e.count)
        e.ops.append((waits, fn, (e.sem, 1)))
        s._record(tok, reads, writes, pwrites)
        return tok

    def dma(s, eng, fn, dbuf, reads=(), writes=(), pwrites=()):
        e = s.engs[eng]
        if dbuf.dsem is None:
            dbuf.dsem = s.stack.enter_context(s.nc.semaphore("ds_" + dbuf.name))
            s.dbufs.append(dbuf)
        waits = s._waits(e, s._deps(e, reads, writes, pwrites))
        dbuf.dcnt += 16
        tok = (dbuf.dsem, dbuf.dcnt)
        e.ops.append((waits, fn, (dbuf.dsem, 16)))
        s._record(tok, reads, writes, pwrites)
        return tok

    def barrier(s):
        toks = [(e.sem, e.count) for e in s.engs.values() if e.count > 0]
        toks += [(b.dsem, b.dcnt) for b in s.dbufs if b.dcnt > 0 and not b.nobar]
        for e in s.engs.values():
            waits = s._waits(e, [t for t in toks if t[0] is not e.sem])
            if waits:
                e.ops.append((waits, None, None))

    def final_wait(s, eng):
        e = s.engs[eng]
        toks = [(x.sem, x.count) for x in s.engs.values() if x.count > 0 and x is not e]
        toks += [(b.dsem, b.dcnt) for b in s.dbufs if b.dcnt > 0]
        waits = s._waits(e, toks)
        e.ops.append((waits, None, None))

    def emit(s):
        nc = s.nc
        with nc.Block() as block:
            def mk(e):
                def body(h):
                    for (waits, fn, inc) in e.ops:
                        for (sem, val) in waits:
                            h.wait_ge(sem, val)
                        if fn is not None:
                            fn(h).then_inc(inc[0], inc[1])
                return body
            block.tensor(mk(s.engs["pe"]))
            block.scalar(mk(s.engs["act"]))
            block.vector(mk(s.engs["dve"]))
            block.gpsimd(mk(s.engs["pool"]))
            block.sync(mk(s.engs["sp"]))


def VW(ap, dims, off=0):
    return bass.AP(ap.tensor, ap.offset + off, [list(ap.ap[0])] + [list(d) for d in dims])


def build_nc(debug=False, stop_after=None):
    nc = bass.Bass("TRN2", target_bir_lowering=False)
    dk = "ExternalOutput" if debug else "Internal"
    x_d = nc.dram_tensor("x", [SEQ, D], F32, kind="ExternalInput").ap()
    c_d = nc.dram_tensor("c", [D], F32, kind="ExternalInput").ap()
    pos_d = nc.dram_tensor("pos", [SEQ], I32, kind="ExternalInput").ap()
    normw_d = nc.dram_tensor("norm_w", [D], F32, kind="ExternalInput").ap()
    wada_d = nc.dram_tensor("w_ada", [D, 3 * D], F32, kind="ExternalInput").ap()
    bada_d = nc.dram_tensor("b_ada", [3 * D], F32, kind="ExternalInput").ap()
    win_d = nc.dram_tensor("w_in", [D, DIN], F32, kind="ExternalInput").ap()
    convw_d = nc.dram_tensor("conv_w", [31, D], F32, kind="ExternalInput").ap()
    convb_d = nc.dram_tensor("conv_b", [D], F32, kind="ExternalInput").ap()
    lnw_d = nc.dram_tensor("conv_ln_w", [D], F32, kind="ExternalInput").ap()
    lnb_d = nc.dram_tensor("conv_ln_b", [D], F32, kind="ExternalInput").ap()
    wco_d = nc.dram_tensor("w_conv_out", [D, D], F32, kind="ExternalInput").ap()
    qw_d = nc.dram_tensor("q_norm_w", [128], F32, kind="ExternalInput").ap()
    kw_d = nc.dram_tensor("k_norm_w", [128], F32, kind="ExternalInput").ap()
    wao_d = nc.dram_tensor("w_attn_out", [512, D], F32, kind="ExternalInput").ap()
    wout_d = nc.dram_tensor("w_out", [D, D], F32, kind="ExternalInput").ap()
    invf_d = nc.dram_tensor("invf", [128], F32, kind="ExternalInput").ap()
    out_d = nc.dram_tensor("out", [SEQ, D], F32, kind="ExternalOutput").ap()
    wbf_d = nc.dram_tensor("wbf", [D, DIN], BF16, kind="Internal").ap()
    hT_d = nc.dram_tensor("hT_d", [128, 8, SEQ], BF16, kind=dk).ap()
    of_d = nc.dram_tensor("of_d", [4, 128, SEQ], BF16, kind=dk).ap()

    with ExitStack() as st:
        S = Sched(nc, st)
        NW = 51900
        big = st.enter_context(nc.sbuf_tensor("big", [128, NW], F32))
        pbh = [st.enter_context(nc.psum_tensor("pb%d" % i, [128, 512], F32)) for i in range(8)]
        pb = [t_[:, :] for t_ in pbh]
        PB = [Buf("pb%d" % i, excl=True) for i in range(8)]
        cur = [0]
        hi = [0]

        def alloc(n, dt=F32):
            if dt == F32:
                r = big[:, cur[0]:cur[0] + n]
                cur[0] += n
            else:
                assert n % 2 == 0
                r = big[:, cur[0]:cur[0] + n // 2].bitcast(BF16)
                cur[0] += n // 2
            hi[0] = max(hi[0], cur[0])
            assert cur[0] <= NW, ("SBUF overflow", cur[0])
            return r

        def act(out, in_, func, reads, writes=(), pwrites=(), scale=1.0, bias=0.0, accum=None):
            if accum is None:
                return S.op("act", lambda h: h.activation(out=out, in_=in_, func=func, bias=bias, scale=scale), reads, writes, pwrites)
            return S.op("act", lambda h: h.activation(out=out, in_=in_, func=func, bias=bias, scale=scale, accum_out=accum), reads, writes, pwrites)

        def tt(eng, out, in0, in1, op, reads, writes=(), pwrites=()):
            return S.op(eng, lambda h: h.tensor_tensor(out=out, in0=in0, in1=in1, op=op), reads, writes, pwrites)

        def ts(eng, out, in0, s1, s2, op0, op1, reads, writes=(), pwrites=()):
            if s2 is None:
                return S.op(eng, lambda h: h.tensor_scalar(out=out, in0=in0, scalar1=s1, scalar2=None, op0=op0), reads, writes, pwrites)
            return S.op(eng, lambda h: h.tensor_scalar(out=out, in0=in0, scalar1=s1, scalar2=s2, op0=op0, op1=op1), reads, writes, pwrites)

        def stt(eng, out, in0, scalar, in1, op0, op1, reads, writes=(), pwrites=()):
            return S.op(eng, lambda h: h.scalar_tensor_tensor(out=out, in0=in0, scalar=scalar, in1=in1, op0=op0, op1=op1), reads, writes, pwrites)

        def cp(eng, out, in_, reads, writes=(), pwrites=()):
            return S.op(eng, lambda h: h.tensor_copy(out=out, in_=in_), reads, writes, pwrites)

        def mms(out, pairs, reads, writes=(), pwrites=()):
            def fn(h):
                n = len(pairs)
                for i, (l, r) in enumerate(pairs):
                    ins = h.matmul(out, lhsT=l, rhs=r, start=(i == 0), stop=(i == n - 1))
                return ins
            return S.op("pe", fn, reads, writes, pwrites)

        def dma(eng, out, in_, dbuf, reads=(), writes=(), pwrites=(), slow=False):
            if slow:
                return S.dma(eng, lambda h: h.dma_start(out=out, in_=in_, allow_slow_non_contiguous=True), dbuf, reads, writes, pwrites)
            return S.dma(eng, lambda h: h.dma_start(out=out, in_=in_), dbuf, reads, writes, pwrites)

        rr = [0]

        def bank(lo=0, hi_=8):
            i = lo + rr[0] % (hi_ - lo)
            rr[0] += 1
            return pb[i], PB[i]

        ident_f = alloc(128); ones_f = alloc(128)
        ident_b = alloc(128, BF16); ones_b = alloc(128, BF16); mask2 = alloc(256, BF16); maskneg = alloc(256, BF16)
        vecs = alloc(64); convw_t = alloc(8 * 31)
        c_t = vecs[:, 0:8]; normw_t = vecs[:, 8:16]; bada_t = vecs[:, 16:40]; convb_t = vecs[:, 40:48]
        lnw_t = vecs[:, 48:56]; lnb_t = vecs[:, 56:64]
        qw_t = alloc(1); kw_t = alloc(1); invf_t = alloc(1)
        eps_t = alloc(1); sgn_t = alloc(1); mod_t = alloc(24); nws_t = alloc(8); sc_t = alloc(8); ssx = alloc(32); rsx = alloc(32)
        gate_bc = alloc(1024)
        B_const = Buf("const"); B_small = Buf("small"); B_mod = Buf("mod"); B_gate = Buf("gatebc")
        B_ssx = Buf("ssx")
        hT_off = cur[0]
        hT = alloc(8 * SEQ, BF16)
        wadaB = big[:, hT_off:hT_off + 8192]
        cosT = alloc(SEQ); sinT = alloc(SEQ)
        B_hT = [Buf("hT%d" % i) for i in range(8)]
        B_tab = Buf("tables")
        B_wbf = [Buf("wbf%d" % i) for i in range(20)]
        for b_ in B_wbf:
            b_.nobar = True
        B_hTd = Buf("hTd"); B_ofd = Buf("ofd")
        base_cur = cur[0]

        cast_secs = list(range(0, 6)) + list(range(16, 20))

        def emit_cast(sidx):
            dma("pool", wbf_d[:, sidx * 512:(sidx + 1) * 512], win_d[:, sidx * 512:(sidx + 1) * 512], B_wbf[sidx], writes=[B_wbf[sidx]])

        S.op("pool", lambda h: h.memset(ident_f, 0.0), writes=[B_const])
        S.op("pool", lambda h: h.affine_select(out=ident_f, in_=ident_f, compare_op=ALU.not_equal, fill=1.0, base=0, pattern=[[-1, 128]], channel_multiplier=1), reads=[B_const], writes=[B_const])
        S.op("pool", lambda h: h.memset(ones_f, 1.0), reads=[B_const], pwrites=[B_const])
        S.op("pool", lambda h: h.memset(ones_b, 1.0), reads=[B_const], pwrites=[B_const])
        cp("pool", ident_b, ident_f, reads=[B_const], pwrites=[B_const])
        S.op("pool", lambda h: h.memset(mask2, 1.0), reads=[B_const], pwrites=[B_const])
        S.op("pool", lambda h: h.affine_select(out=mask2[:, 0:128], in_=mask2[:, 0:128], compare_op=ALU.is_ge, fill=0.0, base=0, pattern=[[1, 128]], channel_multiplier=-1), reads=[B_const], writes=[B_const])
        S.op("pool", lambda h: h.affine_select(out=mask2[:, 128:256], in_=mask2[:, 128:256], compare_op=ALU.is_ge, fill=0.0, base=0, pattern=[[-1, 128]], channel_multiplier=1), reads=[B_const], writes=[B_const])
        ts("pool", maskneg, mask2, -1.0, 30000.0, ALU.add, ALU.mult, reads=[B_const], pwrites=[B_const])
        S.op("pool", lambda h: h.memset(sgn_t[0:64, :], -1.0), reads=[B_const], pwrites=[B_const])
        S.op("pool", lambda h: h.memset(sgn_t[64:128, :], 1.0), reads=[B_const], pwrites=[B_const])
        S.op("pool", lambda h: h.memset(ssx, 0.0), writes=[B_ssx])
        S.op("pool", lambda h: h.memset(eps_t, EPS), reads=[B_const], pwrites=[B_const])
        EPSB = eps_t[:, 0:1]

        B_small.newgen()
        for (t, src) in [(qw_t, qw_d), (kw_t, kw_d), (invf_t, invf_d)]:
            dma("sp", t, src.rearrange("(p o) -> p o", o=1), B_small, pwrites=[B_small], slow=True)
        if stop_after == "P0":
            S.final_wait("sp")
            S.emit()
            return nc
        stg = alloc(128); cwj = alloc(1024)
        B_stg = Buf("stg"); B_cwj = Buf("cwj")
        B_stg.newgen()
        r0 = 0
        for (src, nk) in [(c_d, 8), (normw_d, 8), (bada_d, 24), (convb_d, 8), (lnw_d, 8), (lnb_d, 8)]:
            dma("sp", stg[r0:r0 + nk, :], src.rearrange("(k p) -> k p", p=128), B_stg, pwrites=[B_stg])
            r0 += nk
        dma("sp", cwj[0:31, :], convw_d, B_cwj, writes=[B_cwj])
        S.op("pe", lambda h: h.transpose(out=pb[3][:, 0:64], in_=stg[0:64, :], identity=ident_f[0:64, 0:64]), reads=[B_stg, B_const], writes=[PB[3]])
        cp("dve", vecs, pb[3][:, 0:64], reads=[PB[3]], pwrites=[B_small])

        def cw_tr(h):
            for k in range(8):
                ins = h.transpose(out=pb[2][:, k * 32:k * 32 + 31], in_=cwj[0:31, k * 128:(k + 1) * 128], identity=ident_f[0:31, 0:31])
            return ins
        S.op("pe", cw_tr, reads=[B_cwj, B_const], writes=[PB[2]])
        cp("dve", VW(convw_t, [[31, 8], [1, 31]]), VW(pb[2], [[32, 8], [1, 31]]), reads=[PB[2]], pwrites=[B_small])

        wada = alloc(8 * 1024)
        B_wada = Buf("wada")
        e_t = alloc(8)
        B_e = Buf("e_t")
        act(e_t, c_t, AF.Exp, reads=[B_small], writes=[B_e], scale=-1.0)
        ts("dve", e_t, e_t, 1.0, None, ALU.add, None, reads=[B_e], writes=[B_e])
        S.op("dve", lambda h: h.reciprocal(out=e_t, in_=e_t), reads=[B_e], writes=[B_e])
        tt("dve", sc_t, c_t, e_t, ALU.mult, reads=[B_e, B_small], writes=[B_mod])
        dg_f = alloc(128)
        B_dg = Buf("dg_f")

        B_wadaB = Buf("wadaB")

        def mod_load(jh, buf, BUF):
            BUF.newgen()
            for k in range(8):
                dma("pool", buf[:, k * 1024:(k + 1) * 1024], wada_d[k * 128:(k + 1) * 128, jh * 1024:(jh + 1) * 1024], BUF, pwrites=[BUF])

        def mod_pass(jh, buf, BUF):
            mp, MPB = (pb[0], PB[0]) if jh == 0 else (pb[3], PB[3])

            def mod_mm(h, jh=jh):
                for jj in range(8):
                    for k in range(8):
                        ins = h.matmul(mp[:, jj:jj + 1], lhsT=buf[:, k * 1024 + jj * 128:k * 1024 + (jj + 1) * 128], rhs=sc_t[:, k:k + 1], start=(k == 0), stop=(k == 7))
                return ins
            S.op("pe", mod_mm, reads=[BUF, B_mod], writes=[MPB])
            tt("dve", mod_t[:, jh * 8:(jh + 1) * 8], mp[:, 0:8], bada_t[:, jh * 8:(jh + 1) * 8], ALU.add, reads=[MPB, B_small], pwrites=[B_mod])

        grow = alloc(1024); brow = alloc(1024)
        B_grow = Buf("grow")

        def gate_part():
            dma("sp", brow[0:1, :], bada_d[2048:3072].rearrange("(o n) -> o n", o=1), B_grow, writes=[B_grow])
            for hh in range(2):
                p_, P_ = pb[1 + hh], PB[1 + hh]
                mms(p_[0:1, :], [(sc_t[:, k:k + 1], wada[:, k * 1024 + hh * 512:k * 1024 + (hh + 1) * 512]) for k in range(8)],
                    reads=[B_wada, B_mod], writes=[P_])
                tt("dve", grow[0:1, hh * 512:(hh + 1) * 512], p_[0:1, :], brow[0:1, hh * 512:(hh + 1) * 512], ALU.add, reads=[P_, B_grow], pwrites=[B_grow])
            for hh in range(2):
                p_, P_ = pb[1 + hh], PB[1 + hh]
                mms(p_[:, :], [(ones_f[0:1, :], grow[0:1, hh * 512:(hh + 1) * 512])], reads=[B_grow, B_const], writes=[P_])
                ts("dve", gate_bc[:, hh * 512:(hh + 1) * 512], p_[:, :], 0.5, None, ALU.mult, None, reads=[P_], pwrites=[B_gate])

        if stop_after == "P1":
            S.final_wait("sp")
            S.emit()
            return nc
        INV2PI = float(1.0 / (2 * np.pi)); C1 = 6.28125; C2 = float(2 * np.pi - 6.28125)
        PI = float(np.pi); TWO_PI = float(2 * np.pi)
        B_tab.newgen()

        def mk_tabset(n, tag, share=None):
            d_ = dict(n=n, posi=(alloc(n) if share is None else None), ang=alloc(n), yy=alloc(n))
            if share is None:
                d_["kf"] = alloc(n); d_["mm"] = alloc(n)
                d_["B"] = [Buf(tag + x) for x in ("posi", "ang", "kf", "yy", "mm")]
            else:
                d_["kf"] = share["kf"]; d_["mm"] = share["mm"]
                d_["B"] = [Buf(tag + "posi"), Buf(tag + "ang"), share["B"][2], Buf(tag + "yy"), share["B"][4]]
            return d_
        tabD = mk_tabset(1024, "tD")

        def table_chunk(eng, tb_, c0):
            n = tb_["n"]
            posi, ang, kf, yy, mm_ = tb_["posi"], tb_["ang"], tb_["kf"], tb_["yy"], tb_["mm"]
            B_posi, B_ang, B_kf, B_yy, B_mm = tb_["B"]
            sl = slice(c0, c0 + n)

            def fma(out, a, sc, b, reads, OUT):
                if eng == "dve":
                    stt("dve", out, a, sc, b, ALU.mult, ALU.add, reads=reads, writes=[OUT])
                else:
                    ts(eng, mm_, a, sc, None, ALU.mult, None, reads=reads, writes=[B_mm])
                    tt(eng, out, mm_, b, ALU.add, reads=reads + [B_mm], writes=[OUT])
            if eng == "dve":
                dma("sp", posi.bitcast(I32), pos_d[sl].partition_broadcast(128), B_posi, writes=[B_posi])
                cp(eng, ang, posi.bitcast(I32), reads=[B_posi], writes=[B_ang])
            else:
                cp(eng, ang, posP[:, c0 - 3072:c0 - 3072 + n].bitcast(I32), reads=[B_posP], writes=[B_ang])
            ts(eng, ang, ang, invf_t[:, 0:1], None, ALU.mult, None, reads=[B_ang, B_small], writes=[B_ang])
            ts(eng, kf.bitcast(I32), ang, INV2PI, None, ALU.mult, None, reads=[B_ang], writes=[B_kf])
            cp(eng, kf, kf.bitcast(I32), reads=[B_kf], writes=[B_kf])
            fma(yy, kf, -C1, ang, [B_kf, B_ang], B_yy)
            fma(yy, kf, -C2, yy, [B_kf, B_yy], B_yy)
            ts(eng, kf, yy, PI, None, ALU.is_gt, None, reads=[B_yy], writes=[B_kf])
            fma(yy, kf, -TWO_PI, yy, [B_kf, B_yy], B_yy)
            ts(eng, kf, yy, -PI, None, ALU.is_lt, None, reads=[B_yy], writes=[B_kf])
            fma(yy, kf, TWO_PI, yy, [B_kf, B_yy], B_yy)
            ts(eng, ang, yy, PI / 2, None, ALU.add, None, reads=[B_yy], writes=[B_ang])
            ts(eng, kf, ang, PI, None, ALU.is_gt, None, reads=[B_ang], writes=[B_kf])
            fma(ang, kf, -TWO_PI, ang, [B_kf, B_ang], B_ang)

            def act_part():
                act(sinT[:, sl], yy, AF.Sin, reads=[B_yy, B_const], pwrites=[B_tab], scale=sgn_t[:, 0:1])
                act(cosT[:, sl], ang, AF.Sin, reads=[B_ang], pwrites=[B_tab])
            if eng == "dve":
                act_part()
                return None
            return act_part

        NXB = 5
        xt = [alloc(1024) for _ in range(NXB)]
        xn = [alloc(1024) for _ in range(2)]
        junk = alloc(1024, BF16)
        lnt = alloc(1)
        B_xt = [Buf("xt%d" % i) for i in range(NXB)]
        B_xn = [Buf("xn%d" % i) for i in range(2)]
        B_junk = Buf("junk"); B_lnt = Buf("lnt"); B_rsx = Buf("rsx")
        import os as _os
        NT = int(_os.environ.get('H_TILES', SEQ // 128))
        lnx = alloc(32)
        mod_load(0, wada, B_wada)
        mod_load(1, wadaB, B_wadaB)
        for i in range(NT):
            xb, XB = xt[i % NXB], B_xt[i % NXB]
            dma("sp", xb, x_d[i * 128:(i + 1) * 128, :], XB, writes=[XB])
            act(junk, xb, AF.Square, reads=[XB, B_ssx], writes=[B_junk], accum=ssx[:, i:i + 1])
            if i % 8 == 3:
                table_chunk("dve", tabD, (i // 8) * 1024)
        act(lnx, ssx, AF.Ln, reads=[B_junk, B_ssx], writes=[B_lnt], scale=1.0 / D, bias=EPSB)
        act(rsx, lnx, AF.Exp, reads=[B_lnt], writes=[B_rsx], scale=-0.5)
        mod_pass(0, wada, B_wada)
        mod_load(2, wada, B_wada)
        mod_pass(1, wadaB, B_wadaB)
        stt("dve", nws_t, mod_t[:, 8:16], 1.0, normw_t, ALU.add, ALU.mult, reads=[B_mod, B_small], pwrites=[B_mod])
        gate_part()

        def h_scale(i):
            xb, XB = xt[(NT + i) % NXB], B_xt[(NT + i) % NXB]
            dma("sp", xb, x_d[i * 128:(i + 1) * 128, :], XB, writes=[XB])
            nb_, NB_ = xn[i % 2], B_xn[i % 2]
            act(nb_, xb, AF.Copy, reads=[XB, B_rsx], writes=[NB_], scale=rsx[:, i:i + 1])

        def h_rest(i):
            T = i // 4
            if i % 4 == 0:
                B_hT[T].newgen()
            nb_, NB_ = xn[i % 2], B_xn[i % 2]
            banks = []
            for half in range(2):
                p_, P_ = bank(0, 4)

                def trs(h, half=half, p_=p_, nb_=nb_):
                    for kk in range(4):
                        k = half * 4 + kk
                        ins = h.transpose(out=p_[:, kk * 128:(kk + 1) * 128], in_=nb_[:, k * 128:(k + 1) * 128], identity=ident_f)
                    return ins
                S.op("pe", trs, reads=[NB_, B_const], writes=[P_])
                banks.append((p_, P_))
            for half in range(2):
                p_, P_ = banks[half]
                for kk in range(4):
                    k = half * 4 + kk
                    dst = hT[:, k * SEQ + i * 128:k * SEQ + (i + 1) * 128]
                    if half == 0:
                        act(dst, p_[:, kk * 128:(kk + 1) * 128], AF.Identity, reads=[P_, B_mod], pwrites=[B_hT[T]], scale=nws_t[:, k:k + 1], bias=mod_t[:, k:k + 1])
                    else:
                        ts("dve", dst, p_[:, kk * 128:(kk + 1) * 128], nws_t[:, k:k + 1], mod_t[:, k:k + 1], ALU.mult, ALU.add, reads=[P_, B_mod], pwrites=[B_hT[T]])
            if i % 4 == 3:
                dma("pool", hT_d[:, :, T * 512:(T + 1) * 512], VW(hT, [[SEQ, 8], [1, 512]], off=T * 512), B_hTd, reads=[B_hT[T]], pwrites=[B_hTd])
        if NT > 0:
            h_scale(0)
        for i in range(NT):
            if i + 1 < NT:
                h_scale(i + 1)
            h_rest(i)
        S.barrier()
        cur[0] = base_cur
        if stop_after == "H":
            S.final_wait("sp")
            S.emit()
            return nc

        accU = alloc(SEQ); accZ = alloc(SEQ)
        QT = alloc(SEQ, BF16); KT = alloc(SEQ, BF16); Vh = alloc(32 * 128, BF16)
        wq = [alloc(1024, BF16) for _ in range(2)]
        wk = [alloc(1024, BF16) for _ in range(2)]
        wv = [alloc(1024, BF16) for _ in range(2)]
        wga = alloc(1024, BF16)
        sqb = [alloc(512, BF16) for _ in range(3)]
        NTMP = 9
        tmp = [alloc(512) for _ in range(NTMP)]
        B_tmp = [Buf("tmp%d" % i) for i in range(NTMP)]
        PT = [alloc(256, BF16) for _ in range(4)]
        oft = [alloc(512, BF16) for _ in range(2)]
        B_accU = Buf("accU"); B_accZ = Buf("accZ")
        B_QT = [Buf("QT%d" % i) for i in range(8)]; B_KT = [Buf("KT%d" % i) for i in range(8)]
        B_Vh = Buf("Vh")
        B_wq = [Buf("wq%d" % i) for i in range(2)]; B_wk = [Buf("wk%d" % i) for i in range(2)]; B_wv = [Buf("wv%d" % i) for i in range(2)]
        B_wga = Buf("wga")
        B_sqb = [Buf("sqb%d" % i) for i in range(3)]
        B_PT = [Buf("PT%d" % i) for i in range(4)]
        B_oft = [Buf("oft%d" % i) for i in range(2)]
        tr = [0]

        def gtmp():
            i = tr[0] % NTMP
            tr[0] += 1
            return tmp[i], B_tmp[i]

        def wload(dst, DST, col0):
            dma("pool", VW(dst, [[128, 8], [1, 128]]), win_d[:, col0:col0 + 128].rearrange("(k p) c -> p k c", p=128), DST, writes=[DST])

        heads = [(h, g) for h in range(4) for g in range(3)]
        def load_head_w(idx):
            h, g = heads[idx]
            hd = 4 * g + h
            s_ = idx % 2
            wload(wq[s_], B_wq[s_], 3072 + hd * 128)
            wload(wk[s_], B_wk[s_], 4608 + hd * 128)
            wload(wv[s_], B_wv[s_], 6144 + hd * 128)
        load_head_w(0)
        SC = float(128 ** -0.5)
        item = [0]
        for idx, (h, g) in enumerate(heads):
            win_, dil = GROUPS[g]
            span = dil * 128
            nb = SEQ // span
            ws = idx % 2
            if idx + 1 < len(heads):
                load_head_w(idx + 1)
            if idx < len(cast_secs):
                emit_cast(cast_secs[idx])
            if g == 0:
                wload(wga, B_wga, 7680 + h * 128)
            items = [(T, which) for T in range(8) for which in range(2)]
            st_ = {}

            def p2_a(j):
                T, which = items[j]
                wmat, WB = (wq[ws], B_wq[ws]) if which == 0 else (wk[ws], B_wk[ws])
                p_, P_ = bank(0, 8)
                mms(p_[:, :], [(wmat[:, k * 128:(k + 1) * 128], hT[:, k * SEQ + T * 512:k * SEQ + (T + 1) * 512]) for k in range(8)],
                    reads=[WB, B_hT[T]], writes=[P_])
                sb_, SB_ = sqb[item[0] % 3], B_sqb[item[0] % 3]
                item[0] += 1
                act(sb_, p_[:, :], AF.Square, reads=[P_], writes=[SB_])
                st_[j] = dict(p_=p_, P_=P_, sb_=sb_, SB_=SB_)

            def p2_a2(j):
                d_ = st_[j]
                p2, P2 = bank(0, 8)
                mms(p2[:, :], [(ones_b, d_["sb_"])], reads=[d_["SB_"], B_const], writes=[P2])
                d_.update(p2=p2, P2=P2)

            def p2_b(j):
                T, which = items[j]
                d_ = st_[j]
                wvec = qw_t if which == 0 else kw_t
                rs_, RS_ = gtmp()
                act(rs_, d_["p2"][:, :], AF.Ln, reads=[d_["P2"]], writes=[RS_], scale=1.0 / 128, bias=EPSB)
                act(rs_, rs_, AF.Exp, reads=[RS_], writes=[RS_], scale=-0.5)
                qn, QN = gtmp()
                stt("dve", qn, d_["p_"][:, :], wvec[:, 0:1], rs_, ALU.mult, ALU.mult, reads=[d_["P_"], RS_, B_small], writes=[QN])
                d_.update(rs_=rs_, RS_=RS_, qn=qn, QN=QN)

            def p2_c(j):
                T, which = items[j]
                d_ = st_.pop(j)
                tsl = slice(T * 512, (T + 1) * 512)
                dstT, DB = (QT, B_QT[T]) if which == 0 else (KT, B_KT[T])
                rs_, RS_, qn, QN = d_["rs_"], d_["RS_"], d_["qn"], d_["QN"]
                qs, QS = gtmp()
                QS.newgen()
                act(qs[0:64, :], qn[64:128, :], AF.Copy, reads=[QN], pwrites=[QS])
                act(qs[64:128, :], qn[0:64, :], AF.Copy, reads=[QN], pwrites=[QS])
                tt("dve", rs_, qn, cosT[:, tsl], ALU.mult, reads=[QN, B_tab], writes=[RS_])
                tt("pool", qs, qs, sinT[:, tsl], ALU.mult, reads=[QS, B_tab], writes=[QS])
                tt("pool", dstT[:, tsl], rs_, qs, ALU.add, reads=[RS_, QS], writes=[DB])

            nit = len(items)
            for j in range(nit + 3):
                if j < nit:
                    p2_a(j)
                if 0 <= j - 1 < nit:
                    p2_a2(j - 1)
                if 0 <= j - 2 < nit:
                    p2_b(j - 2)
                if 0 <= j - 3 < nit:
                    p2_c(j - 3)
            blocks = [(r, n) for r in range(dil) for n in range(nb)]
            B_Vh.newgen()
            for b0 in range(0, 32, 4):
                p_, P_ = bank(0, 4)
                P_.newgen()
                for s_ in range(4):
                    r, n = blocks[b0 + s_]
                    Tt = (n * span) // 512
                    rd = [B_hT[t_] for t_ in range((n * span) // 512, min(8, ((n + 1) * span + 511) // 512))]
                    mms(p_[:, s_ * 128:(s_ + 1) * 128],
                        [(VW(hT, [[dil, 128]], off=k * SEQ + n * span + r), wv[ws][:, k * 128:(k + 1) * 128]) for k in range(8)],
                        reads=[B_wv[ws]] + rd, pwrites=[P_])
                if (b0 // 4) % 2 == 0:
                    act(Vh[:, b0 * 128:(b0 + 4) * 128], p_[:, :], AF.Copy, reads=[P_], pwrites=[B_Vh])
                else:
                    cp("dve", Vh[:, b0 * 128:(b0 + 4) * 128], p_[:, :], reads=[P_], pwrites=[B_Vh])
            nbat = 4 if nb >= 4 else nb
            pts = {}

            def p4_s(bi):
                r, n = blocks[bi]
                last = (n == nb - 1)
                nq = 1 if last else 2
                kT_tiles = [B_KT[t_] for t_ in range((n * span) // 512, min(8, ((n + 1) * span + 511) // 512))]
                qT_tiles = [B_QT[t_] for t_ in range((n * span) // 512, min(8, ((n + nq) * span + 511) // 512))]
                p_, P_ = bank(0, 4)
                mms(p_[:, 0:nq * 128], [(VW(KT, [[dil, 128]], off=n * span + r), VW(QT, [[span, nq], [dil, 128]], off=n * span + r)),
                                         (ident_b, maskneg[:, 0:nq * 128])],
                    reads=kT_tiles + qT_tiles + [B_const], writes=[P_])
                pt_, PT_ = PT[bi % 4], B_PT[bi % 4]
                act(pt_[:, 0:nq * 128], p_[:, 0:nq * 128], AF.Exp, reads=[P_], writes=[PT_], scale=SC)
                pts[bi] = (pt_, PT_)

            def p4_pv(bi):
                r, n = blocks[bi]
                pt_, PT_ = pts[bi]
                s_ = n % nbat
                if s_ == 0:
                    pts["u"] = (pb[4 + (bi // nbat) % 2], PB[4 + (bi // nbat) % 2], pb[6 + (bi // nbat) % 2], PB[6 + (bi // nbat) % 2])
                    pts["u"][1].newgen(); pts["u"][3].newgen()
                up, UP, zp, ZP = pts["u"]
                pairs_u = []
                pairs_z = []
                rds = [PT_, B_Vh, B_const]
                if n > 0:
                    ppt, PPT = pts[bi - 1]
                    pairs_u.append((Vh[:, (bi - 1) * 128:bi * 128], ppt[:, 128:256]))
                    pairs_z.append((ones_b, ppt[:, 128:256]))
                    rds.append(PPT)
                pairs_u.append((Vh[:, bi * 128:(bi + 1) * 128], pt_[:, 0:128]))
                pairs_z.append((ones_b, pt_[:, 0:128]))
                mms(up[:, s_ * 128:(s_ + 1) * 128], pairs_u, reads=rds, pwrites=[UP])
                mms(zp[:, s_ * 128:(s_ + 1) * 128], pairs_z, reads=rds, pwrites=[ZP])
                if bi - 1 in pts and bi >= 1:
                    pass
                if s_ == nbat - 1:
                    n0 = n - (nbat - 1)
                    vdims = [[span, nbat], [dil, 128]]
                    voff = n0 * span + r
                    au = VW(accU, vdims, off=voff)
                    az = VW(accZ, vdims, off=voff)
                    pu = VW(up, [[128, nbat], [1, 128]])
                    pz = VW(zp, [[128, nbat], [1, 128]])
                    if g == 0:
                        act(au, pu, AF.Copy, reads=[UP], pwrites=[B_accU])
                        cp("dve", az, pz, reads=[ZP], pwrites=[B_accZ])
                    else:
                        tt("dve", au, pu, au, ALU.add, reads=[UP, B_accU], pwrites=[B_accU])
                        tt("dve", az, pz, az, ALU.add, reads=[ZP, B_accZ], pwrites=[B_accZ])

            nblk = len(blocks)
            for bi in range(nblk + 1):
                if bi < nblk:
                    p4_s(bi)
                if bi >= 1:
                    p4_pv(bi - 1)
                    pts.pop(bi - 3, None)
            if g == 2:
                fin = {}

                def fin_a(T):
                    tsl = slice(T * 512, (T + 1) * 512)
                    p_, P_ = bank(0, 4)
                    mms(p_[:, :], [(wga[:, k * 128:(k + 1) * 128], hT[:, k * SEQ + T * 512:k * SEQ + (T + 1) * 512]) for k in range(8)],
                        reads=[B_wga, B_hT[T]], writes=[P_])
                    e_, E_ = gtmp()
                    act(e_, p_[:, :], AF.Exp, reads=[P_], writes=[E_], scale=-1.0)
                    num, NUM = gtmp()
                    tt("dve", num, p_[:, :], accU[:, tsl], ALU.mult, reads=[P_, B_accU], writes=[NUM])
                    stt("dve", e_, e_, 1.0, accZ[:, tsl], ALU.add, ALU.mult, reads=[E_, B_accZ], writes=[E_])
                    fin[T] = (e_, E_, num, NUM)

                def fin_b(T):
                    tsl = slice(T * 512, (T + 1) * 512)
                    e_, E_, num, NUM = fin.pop(T)
                    act(e_, e_, AF.Ln, reads=[E_], writes=[E_])
                    act(e_, e_, AF.Exp, reads=[E_], writes=[E_], scale=-1.0)
                    ob, OB = oft[T % 2], B_oft[T % 2]
                    tt("pool", ob, num, e_, ALU.mult, reads=[NUM, E_], writes=[OB])
                    dma("pool", of_d[h][:, tsl], ob, OB, reads=[OB], pwrites=[B_ofd])
                for T in range(9):
                    if T < 8:
                        fin_a(T)
                    if T >= 1:
                        fin_b(T - 1)
                B_accU.newgen(); B_accZ.newgen()
        S.barrier()
        cur[0] = base_cur
        if stop_after == "1":
            S.final_wait("sp")
            S.emit()
            return nc

        cur[0] = base_cur - (8 * SEQ // 2 + 2 * SEQ)
        wco = alloc(8 * 1024, BF16); wao = alloc(4 * 1024, BF16); wout = alloc(8 * 1024, BF16)
        B_wres = Buf("wres")
        B_wres.newgen()

        def load_wres(after):
            dma("pool", VW(wco, [[1024, 8], [1, 1024]]), wco_d.rearrange("(k p) n -> p k n", p=128), B_wres, reads=after, pwrites=[B_wres])
            dma("pool", VW(wao, [[1024, 4], [1, 1024]]), wao_d.rearrange("(k p) n -> p k n", p=128), B_wres, reads=after, pwrites=[B_wres])
            dma("pool", VW(wout, [[1024, 8], [1, 1024]]), wout_d.rearrange("(k p) n -> p k n", p=128), B_wres, reads=after, pwrites=[B_wres])
        NRING = 8
        ring = [alloc(1024, BF16) for _ in range(NRING)]
        B_ring = [Buf("ring%d" % i) for i in range(NRING)]
        hTt = [alloc(8 * 512, BF16) for _ in range(2)]
        B_hTt = [Buf("hTt%d" % i) for i in range(2)]
        oft2 = [alloc(4 * 512, BF16) for _ in range(2)]
        B_oft2 = [Buf("oft2_%d" % i) for i in range(2)]
        upad = [alloc(544, BF16) for _ in range(2)]
        B_upad = [Buf("upad%d" % i) for i in range(2)]
        halo = alloc(8 * 32, BF16)
        B_halo = [Buf("halo%d" % i) for i in range(8)]
        dgt = [alloc(31 * 128, BF16) for _ in range(3)]
        B_dgt = [Buf("dgt%d" % i) for i in range(3)]
        uc = alloc(8 * 512)
        B_uc = [Buf("uc%d" % i) for i in range(8)]
        NTB = 4
        tb = [alloc(512, BF16) for _ in range(NTB)]
        B_tb = [Buf("tb%d" % i) for i in range(NTB)]
        NT2 = 12
        tmp2 = [alloc(512) for _ in range(NT2)]
        B_tmp2 = [Buf("tmp2_%d" % i) for i in range(NT2)]
        stat = [alloc(512) for _ in range(3)]
        B_stat = [Buf("stat%d" % i) for i in range(3)]
        ufin = alloc(8 * 512, BF16)
        B_ufin = [Buf("ufin%d" % i) for i in range(8)]
        yb = alloc(8 * 512, BF16)
        B_yb = [Buf("yb%d" % i) for i in range(8)]
        xres = [alloc(1024) for _ in range(2)]
        B_xres = [Buf("xres%d" % i) for i in range(2)]
        res = [alloc(1024) for _ in range(2)]
        B_res = [Buf("res%d" % i) for i in range(2)]
        t2c = [0]
        tbc = [0]

        def gt2():
            i = t2c[0] % NT2
            t2c[0] += 1
            return tmp2[i], B_tmp2[i]

        def gtb():
            i = tbc[0] % NTB
            tbc[0] += 1
            return tb[i], B_tb[i]

        def col_of(ci):
            sec, c = ci // 8, ci % 8
            base = [0, 1024, 2048, 8192, 9216][sec]
            return base + c * 128
        order = []
        for c in range(8):
            order += [0 * 8 + c, 1 * 8 + c]
        order += [2 * 8 + c for c in range(8)]
        for f in range(8):
            order += [3 * 8 + f, 4 * 8 + f]
        stream = [(t, ci) for t in range(8) for ci in order]
        issued = [0]
        used = [0]

        def wnext():
            while issued[0] < len(stream) and issued[0] < used[0] + NRING:
                j = issued[0]
                t_, ci = stream[j]
                col0 = col_of(ci)
                dma("sp", VW(ring[j % NRING], [[128, 8], [1, 128]]), wbf_d[:, col0:col0 + 128].rearrange("(k p) c -> p k c", p=128),
                    B_ring[j % NRING], reads=[B_wbf[col0 // 512]], writes=[B_ring[j % NRING]])
                issued[0] += 1
            j = used[0]
            used[0] += 1
            return ring[j % NRING], B_ring[j % NRING], stream[j][1]

        def proj(p_, P_, hb, HB):
            w_, W_, ci = wnext()
            mms(p_[:, :], [(w_[:, k * 128:(k + 1) * 128], hb[:, k * 512:(k + 1) * 512]) for k in range(8)], reads=[W_, HB], writes=[P_])
            return ci

        s1p, S1P = pb[6], PB[6]
        s2p, S2P = pb[7], PB[7]

        def tile_loads(t_):
            sl_ = slice(t_ * 512, (t_ + 1) * 512)
            dma("sp", VW(hTt[t_ % 2], [[512, 8], [1, 512]]), hT_d[:, :, sl_], B_hTt[t_ % 2], reads=[B_hTd], writes=[B_hTt[t_ % 2]])
            dma("sp", VW(oft2[t_ % 2], [[512, 4], [1, 512]]), of_d[:, :, sl_].rearrange("s p t -> p s t"), B_oft2[t_ % 2], reads=[B_ofd], writes=[B_oft2[t_ % 2]])
        for t in range(8):
            tsl = slice(t * 512, (t + 1) * 512)
            hb, HB = hTt[t % 2], B_hTt[t % 2]
            ofb, OFB = oft2[t % 2], B_oft2[t % 2]
            if t == 0:
                tile_loads(0)
            S1P.newgen(); S2P.newgen()
            pend_st = []
            def do_diag(gi):
                c = gi % 8
                db, DB_ = dgt[gi % 3], B_dgt[gi % 3]
                tt("dve", VW(db, [[128, 31], [1, 128]]), VW(ident_b, [[0, 31], [1, 128]]), VW(convw_t, [[1, 31], [0, 128]], off=c * 31), ALU.mult,
                   reads=[B_const, B_small], writes=[DB_])

            def do_ab(t_, c):
                hb_, HB_ = hTt[t_ % 2], B_hTt[t_ % 2]
                ub, UB = upad[c % 2], B_upad[c % 2]
                pa, PA = bank(0, 6)
                ci = proj(pa, PA, hb_, HB_)
                assert ci == c
                pbb, PBB = bank(0, 6)
                ci = proj(pbb, PBB, hb_, HB_)
                assert ci == 8 + c
                th, TH = gt2()
                act(th, pbb[:, :], AF.Tanh, reads=[PBB], writes=[TH], scale=0.5)
                UB.newgen()
                if t_ == 0:
                    S.op("pool", (lambda ub: lambda h: h.memset(ub[:, 0:30], 0.0))(ub), pwrites=[UB])
                else:
                    cp("pool", ub[:, 0:30], halo[:, c * 32:c * 32 + 30], reads=[B_halo[c]], pwrites=[UB])
                stt("dve", ub[:, 30:542], th, 1.0, pa[:, :], ALU.add, ALU.mult, reads=[TH, PA], pwrites=[UB])
                cp("pool", halo[:, c * 32:c * 32 + 30], ub[:, 512:542], reads=[UB], writes=[B_halo[c]])

            def do_conv(c):
                gi = t * 8 + c
                ub, UB = upad[c % 2], B_upad[c % 2]
                db, DB_ = dgt[gi % 3], B_dgt[gi % 3]
                pc, PC = bank(0, 6)
                mms(pc[:, :], [(db[:, j * 128:(j + 1) * 128], ub[:, j:j + 512]) for j in range(31)], reads=[UB, DB_], writes=[PC])
                act(uc[:, c * 512:(c + 1) * 512], pc[:, :], AF.Identity, reads=[PC, B_small], writes=[B_uc[c]], scale=0.5, bias=convb_t[:, c:c + 1])
                sq_, SQ_ = gtb()
                act(sq_, pc[:, :], AF.Square, reads=[PC, B_small], writes=[SQ_], scale=0.5, bias=convb_t[:, c:c + 1])
                ucb, UCB = gtb()
                cp("pool", ucb, uc[:, c * 512:(c + 1) * 512], reads=[B_uc[c]], writes=[UCB])

                def st_mm(h, c=c, ucb=ucb, sq_=sq_):
                    h.matmul(s1p[:, :], lhsT=ones_b, rhs=ucb, start=(c == 0), stop=(c == 7))
                    return h.matmul(s2p[:, :], lhsT=ones_b, rhs=sq_, start=(c == 0), stop=(c == 7))
                pend_st.append(lambda: S.op("pe", st_mm, reads=[UCB, SQ_, B_const], pwrites=[S1P, S2P]))

            if t == 0:
                do_diag(0)
                do_diag(1)
                do_ab(0, 0)
                load_wres([B_upad[0]])
            for c in range(8):
                if c + 1 < 8:
                    do_ab(t, c + 1)
                if t * 8 + c + 2 < 64:
                    do_diag(t * 8 + c + 2)
                do_conv(c)
                if len(pend_st) > 1:
                    pend_st.pop(0)()
            while pend_st:
                pend_st.pop(0)()
            mean, MEAN = stat[0], B_stat[0]
            rstd, RSTD = stat[1], B_stat[1]
            mr, MR = stat[2], B_stat[2]
            ts("dve", mean, s1p[:, :], 1.0 / D, None, ALU.mult, None, reads=[S1P], writes=[MEAN])
            tt("pool", mr, mean, mean, ALU.mult, reads=[MEAN], writes=[MR])
            stt("dve", rstd, s2p[:, :], 1.0 / D, mr, ALU.mult, ALU.subtract, reads=[S2P, MR], writes=[RSTD])
            act(rstd, rstd, AF.Ln, reads=[RSTD], writes=[RSTD], bias=EPSB)
            act(rstd, rstd, AF.Exp, reads=[RSTD], writes=[RSTD], scale=-0.5)
            tt("pool", mr, mean, rstd, ALU.mult, reads=[MEAN, RSTD], writes=[MR])
            for c in range(8):
                pg, PG = bank(0, 6)
                ci = proj(pg, PG, hb, HB)
                assert ci == 16 + c
                sg, SG = gt2()
                act(sg, pg[:, :], AF.Silu, reads=[PG], writes=[SG])
                t_, T_ = gt2()
                tt("dve", t_, uc[:, c * 512:(c + 1) * 512], rstd, ALU.mult, reads=[B_uc[c], RSTD], writes=[T_])
                tt("dve", t_, t_, mr, ALU.subtract, reads=[T_, MR], writes=[T_])
                act(t_, t_, AF.Silu, reads=[T_, B_small], writes=[T_], scale=lnw_t[:, c:c + 1], bias=lnb_t[:, c:c + 1])
                tt("dve", ufin[:, c * 512:(c + 1) * 512], t_, sg, ALU.mult, reads=[T_, SG], writes=[B_ufin[c]])
            if t + 1 < 8:
                tile_loads(t + 1)
            pend_yc = []
            for f in range(8):
                pm, PM = bank(0, 8)
                ci = proj(pm, PM, hb, HB)
                assert ci == 24 + f
                tmc, TMC = gt2()
                act(tmc, pm[:, :], AF.Tanh, reads=[PM], writes=[TMC], scale=0.5)
                pm2, PM2 = bank(0, 8)
                ci = proj(pm2, PM2, hb, HB)
                assert ci == 32 + f
                tma, TMA = gt2()
                act(tma, pm2[:, :], AF.Tanh, reads=[PM2], writes=[TMA], scale=0.5)
                pa_, PA_ = bank(0, 8)
                mms(pa_[:, :], [(wao[:, s_ * 1024 + f * 128:s_ * 1024 + (f + 1) * 128], ofb[:, s_ * 512:(s_ + 1) * 512]) for s_ in range(4)],
                    reads=[B_wres, OFB], writes=[PA_])
                stt("dve", tma, tma, 1.0, pa_[:, :], ALU.add, ALU.mult, reads=[TMA, PA_], writes=[TMA])

                def yc_part(f=f, tmc=tmc, TMC=TMC, tma=tma, TMA=TMA):
                    pc_, PC_ = bank(0, 8)
                    mms(pc_[:, :], [(wco[:, c * 1024 + f * 128:c * 1024 + (f + 1) * 128], ufin[:, c * 512:(c + 1) * 512]) for c in range(8)],
                        reads=[B_wres] + B_ufin, writes=[PC_])
                    stt("dve", tmc, tmc, 1.0, pc_[:, :], ALU.add, ALU.mult, reads=[TMC, PC_], writes=[TMC])
                    tt("pool", yb[:, f * 512:(f + 1) * 512], tmc, tma, ALU.add, reads=[TMC, TMA], writes=[B_yb[f]])
                pend_yc.append(yc_part)
                if len(pend_yc) > 1:
                    pend_yc.pop(0)()
            while pend_yc:
                pend_yc.pop(0)()
            if t + 1 < 8:
                do_ab(t + 1, 0)
            for s_ in range(4):
                row0 = t * 512 + s_ * 128
                xb, XB = xres[s_ % 2], B_xres[s_ % 2]
                dma("sp", xb, x_d[row0:row0 + 128, :], XB, writes=[XB])
                rb, RB = res[s_ % 2], B_res[s_ % 2]
                RB.newgen()
                for half in range(2):
                    po, PO = bank(0, 6)
                    mms(po[:, :], [(yb[:, f * 512 + s_ * 128:f * 512 + (s_ + 1) * 128], wout[:, f * 1024 + half * 512:f * 1024 + (half + 1) * 512]) for f in range(8)],
                        reads=[B_wres] + B_yb, writes=[PO])
                    hs = slice(half * 512, (half + 1) * 512)
                    tt("dve", rb[:, hs], po[:, :], gate_bc[:, hs], ALU.mult, reads=[PO, B_gate], pwrites=[RB])
                    tt("pool", rb[:, hs], rb[:, hs], xb[:, hs], ALU.add, reads=[RB, XB], pwrites=[RB])
                dma("pool", out_d[row0:row0 + 128, :], rb, RB, reads=[RB])
        S.final_wait("sp")
        S.emit()
        print("SBUF words hi", hi[0], "instr counts", {k: len(v.ops) for k, v in S.engs.items()})
    return nc


_NC_CACHE = {}


def _make_in_maps(x, c, positions, norm_w, w_ada, b_ada, w_in, conv_w, conv_b, conv_ln_w, conv_ln_b,
                  w_conv_out, q_norm_w, k_norm_w, w_attn_out, w_out):
    f = lambda a: np.ascontiguousarray(np.asarray(a, dtype=np.float32))
    inv = (10000.0 ** (-np.arange(0, 128, 2, dtype=np.float32) / 128)).astype(np.float32)
    invf = np.concatenate([inv, inv]).astype(np.float32)
    shared = {
        "norm_w": f(norm_w), "w_ada": f(w_ada), "b_ada": f(b_ada), "w_in": f(w_in), "conv_w": f(conv_w),
        "conv_b": f(conv_b), "conv_ln_w": f(conv_ln_w), "conv_ln_b": f(conv_ln_b), "w_conv_out": f(w_conv_out),
        "q_norm_w": f(q_norm_w), "k_norm_w": f(k_norm_w), "w_attn_out": f(w_attn_out), "w_out": f(w_out), "invf": invf,
    }
    x = np.asarray(x, dtype=np.float32)
    c = np.asarray(c, dtype=np.float32)
    positions = np.asarray(positions, dtype=np.int32)
    in_maps = []
    for b in range(NCORE):
        m = dict(shared)
        m["x"] = np.ascontiguousarray(x[b])
        m["c"] = np.ascontiguousarray(c[b])
        m["pos"] = np.ascontiguousarray(positions[b])
        in_maps.append(m)
    return in_maps


def kernel(**inputs):
    in_maps = _make_in_maps(**inputs)
    if "nc" not in _NC_CACHE:
        _NC_CACHE["nc"] = build_nc()
    nc = _NC_CACHE["nc"]
    res = run_bass_kernel_spmd(nc, in_maps, core_ids=list(range(NCORE)))
    out = np.stack([np.asarray(res.results[b]["out"], dtype=np.float32) for b in range(NCORE)], axis=0)
    return out
```

```python
import numpy as np
from contextlib import ExitStack
import concourse.bass as bass
import concourse.mybir as mybir
from concourse.bass_utils import run_bass_kernel_spmd

F32 = mybir.dt.float32
BF16 = mybir.dt.bfloat16
I32 = mybir.dt.int32
AF = mybir.ActivationFunctionType
ALU = mybir.AluOpType

SEQ = 4096
D = 1024
DIN = 10240
NCORE = 8
EPS = 1e-6
GROUPS = ((128, 1), (512, 4), (2048, 16))
SAME_ENGINE_SYNC = True


class Buf:
    def __init__(s, name, excl=False):
        s.name = name
        s.nobar = False
        s.excl = excl
        s.w = []
        s.r = []
        s.gd = []
        s.dsem = None
        s.dcnt = 0

    def newgen(s):
        s.gd = s.w + s.r
        s.w = []
        s.r = []


class Eng:
    def __init__(s, name, sem):
        s.name = name
        s.sem = sem
        s.count = 0
        s.ops = []
        s.waited = {}


class Sched:
    def __init__(s, nc, stack):
        s.nc = nc
        s.stack = stack
        s.engs = {}
        for n in ["pe", "act", "dve", "pool", "sp"]:
            s.engs[n] = Eng(n, stack.enter_context(nc.semaphore("es_" + n)))
        s.dbufs = []

    def _waits(s, e, deps):
        need = {}
        for (sem, val) in deps:
            if sem is e.sem and not SAME_ENGINE_SYNC:
                continue
            k = sem.num
            if k not in need or val > need[k][1]:
                need[k] = (sem, val)
        out = []
        for k, (sem, val) in need.items():
            if e.waited.get(k, 0) >= val:
                continue
            e.waited[k] = val
            out.append((sem, val))
        return out

    @staticmethod
    def _deps(e, reads, writes, pwrites):
        deps = []
        for b in reads:
            deps += b.w
            if b.excl:
                deps += [t for t in b.r if t[0] is not e.sem]
        for b in writes:
            deps += b.gd + b.w + b.r
        for b in pwrites:
            deps += b.gd + b.r
        return deps

    @staticmethod
    def _record(tok, reads, writes, pwrites):
        for b in reads:
            b.r.append(tok)
        for b in writes:
            b.gd = b.w + b.r
            b.w = [tok]
            b.r = []
        for b in pwrites:
            b.w.append(tok)

    def op(s, eng, fn, reads=(), writes=(), pwrites=()):
        e = s.engs[eng]
        waits = s._waits(e, s._deps(e, reads, writes, pwrites))
        e.count += 1
        tok = (e.sem, e.count)
        e.ops.append((waits, fn, (e.sem, 1)))
        s._record(tok, reads, writes, pwrites)
        return tok

    def dma(s, eng, fn, dbuf, reads=(), writes=(), pwrites=()):
        e = s.engs[eng]
        if dbuf.dsem is None:
            dbuf.dsem = s.stack.enter_context(s.nc.semaphore("ds_" + dbuf.name))
            s.dbufs.append(dbuf)
        waits = s._waits(e, s._deps(e, reads, writes, pwrites))
        dbuf.dcnt += 16
        tok = (dbuf.dsem, dbuf.dcnt)
        e.ops.append((waits, fn, (dbuf.dsem, 16)))
        s._record(tok, reads, writes, pwrites)
        return tok

    def barrier(s):
        toks = [(e.sem, e.count) for e in s.engs.values() if e.count > 0]
        toks += [(b.dsem, b.dcnt) for b in s.dbufs if b.dcnt > 0 and not b.nobar]
        for e in s.engs.values():
            waits = s._waits(e, [t for t in toks if t[0] is not e.sem])
            if waits:
                e.ops.append((waits, None, None))

    def final_wait(s, eng):
        e = s.engs[eng]
        toks = [(x.sem, x.count) for x in s.engs.values() if x.count > 0 and x is not e]
        toks += [(b.dsem, b.dcnt) for b in s.dbufs if b.dcnt > 0]
        waits = s._waits(e, toks)
        e.ops.append((waits, None, None))

    def emit(s):
        nc = s.nc
        with nc.Block() as block:
            def mk(e):
                def body(h):
                    for (waits, fn, inc) in e.ops:
                        for (sem, val) in waits:
                            h.wait_ge(sem, val)
                        if fn is not None:
                            fn(h).then_inc(inc[0], inc[1])
                return body
            block.tensor(mk(s.engs["pe"]))
            block.scalar(mk(s.engs["act"]))
            block.vector(mk(s.engs["dve"]))
            block.gpsimd(mk(s.engs["pool"]))
            block.sync(mk(s.engs["sp"]))


def VW(ap, dims, off=0):
    return bass.AP(ap.tensor, ap.offset + off, [list(ap.ap[0])] + [list(d) for d in dims])


def build_nc(debug=False, stop_after=None):
    nc = bass.Bass("TRN2", target_bir_lowering=False)
    dk = "ExternalOutput" if debug else "Internal"
    x_d = nc.dram_tensor("x", [SEQ, D], F32, kind="ExternalInput").ap()
    c_d = nc.dram_tensor("c", [D], F32, kind="ExternalInput").ap()
    pos_d = nc.dram_tensor("pos", [SEQ], I32, kind="ExternalInput").ap()
    normw_d = nc.dram_tensor("norm_w", [D], F32, kind="ExternalInput").ap()
    wada_d = nc.dram_tensor("w_ada", [D, 3 * D], F32, kind="ExternalInput").ap()
    bada_d = nc.dram_tensor("b_ada", [3 * D], F32, kind="ExternalInput").ap()
    win_d = nc.dram_tensor("w_in", [D, DIN], F32, kind="ExternalInput").ap()
    convw_d = nc.dram_tensor("conv_w", [31, D], F32, kind="ExternalInput").ap()
    convb_d = nc.dram_tensor("conv_b", [D], F32, kind="ExternalInput").ap()
    lnw_d = nc.dram_tensor("conv_ln_w", [D], F32, kind="ExternalInput").ap()
    lnb_d = nc.dram_tensor("conv_ln_b", [D], F32, kind="ExternalInput").ap()
    wco_d = nc.dram_tensor("w_conv_out", [D, D], F32, kind="ExternalInput").ap()
    qw_d = nc.dram_tensor("q_norm_w", [128], F32, kind="ExternalInput").ap()
    kw_d = nc.dram_tensor("k_norm_w", [128], F32, kind="ExternalInput").ap()
    wao_d = nc.dram_tensor("w_attn_out", [512, D], F32, kind="ExternalInput").ap()
    wout_d = nc.dram_tensor("w_out", [D, D], F32, kind="ExternalInput").ap()
    invf_d = nc.dram_tensor("invf", [128], F32, kind="ExternalInput").ap()
    out_d = nc.dram_tensor("out", [SEQ, D], F32, kind="ExternalOutput").ap()
    wbf_d = nc.dram_tensor("wbf", [D, DIN], BF16, kind="Internal").ap()
    hT_d = nc.dram_tensor("hT_d", [128, 8, SEQ], BF16, kind=dk).ap()
    of_d = nc.dram_tensor("of_d", [4, 128, SEQ], BF16, kind=dk).ap()

    with ExitStack() as st:
        S = Sched(nc, st)
        NW = 51900
        big = st.enter_context(nc.sbuf_tensor("big", [128, NW], F32))
        pbh = [st.enter_context(nc.psum_tensor("pb%d" % i, [128, 512], F32)) for i in range(8)]
        pb = [t_[:, :] for t_ in pbh]
        PB = [Buf("pb%d" % i, excl=True) for i in range(8)]
        cur = [0]
        hi = [0]

        def alloc(n, dt=F32):
            if dt == F32:
                r = big[:, cur[0]:cur[0] + n]
                cur[0] += n
            else:
                assert n % 2 == 0
                r = big[:, cur[0]:cur[0] + n // 2].bitcast(BF16)
                cur[0] += n // 2
            hi[0] = max(hi[0], cur[0])
            assert cur[0] <= NW, ("SBUF overflow", cur[0])
            return r

        def act(out, in_, func, reads, writes=(), pwrites=(), scale=1.0, bias=0.0, accum=None):
            if accum is None:
                return S.op("act", lambda h: h.activation(out=out, in_=in_, func=func, bias=bias, scale=scale), reads, writes, pwrites)
            return S.op("act", lambda h: h.activation(out=out, in_=in_, func=func, bias=bias, scale=scale, accum_out=accum), reads, writes, pwrites)

        def tt(eng, out, in0, in1, op, reads, writes=(), pwrites=()):
            return S.op(eng, lambda h: h.tensor_tensor(out=out, in0=in0, in1=in1, op=op), reads, writes, pwrites)

        def ts(eng, out, in0, s1, s2, op0, op1, reads, writes=(), pwrites=()):
            if s2 is None:
                return S.op(eng, lambda h: h.tensor_scalar(out=out, in0=in0, scalar1=s1, scalar2=None, op0=op0), reads, writes, pwrites)
            return S.op(eng, lambda h: h.tensor_scalar(out=out, in0=in0, scalar1=s1, scalar2=s2, op0=op0, op1=op1), reads, writes, pwrites)

        def stt(eng, out, in0, scalar, in1, op0, op1, reads, writes=(), pwrites=()):
            return S.op(eng, lambda h: h.scalar_tensor_tensor(out=out, in0=in0, scalar=scalar, in1=in1, op0=op0, op1=op1), reads, writes, pwrites)

        def cp(eng, out, in_, reads, writes=(), pwrites=()):
            return S.op(eng, lambda h: h.tensor_copy(out=out, in_=in_), reads, writes, pwrites)

        def mms(out, pairs, reads, writes=(), pwrites=()):
            def fn(h):
                n = len(pairs)
                for i, (l, r) in enumerate(pairs):
                    ins = h.matmul(out, lhsT=l, rhs=r, start=(i == 0), stop=(i == n - 1))
                return ins
            return S.op("pe", fn, reads, writes, pwrites)

        def dma(eng, out, in_, dbuf, reads=(), writes=(), pwrites=(), slow=False):
            if slow:
                return S.dma(eng, lambda h: h.dma_start(out=out, in_=in_, allow_slow_non_contiguous=True), dbuf, reads, writes, pwrites)
            return S.dma(eng, lambda h: h.dma_start(out=out, in_=in_), dbuf, reads, writes, pwrites)

        rr = [0]

        def bank(lo=0, hi_=8):
            i = lo + rr[0] % (hi_ - lo)
            rr[0] += 1
            return pb[i], PB[i]

        ident_f = alloc(128); ones_f = alloc(128)
        ident_b = alloc(128, BF16); ones_b = alloc(128, BF16); mask2 = alloc(256, BF16); maskneg = alloc(256, BF16)
        vecs = alloc(64); convw_t = alloc(8 * 31)
        c_t = vecs[:, 0:8]; normw_t = vecs[:, 8:16]; bada_t = vecs[:, 16:40]; convb_t = vecs[:, 40:48]
        lnw_t = vecs[:, 48:56]; lnb_t = vecs[:, 56:64]
        qw_t = alloc(1); kw_t = alloc(1); invf_t = alloc(1)
        eps_t = alloc(1); sgn_t = alloc(1); mod_t = alloc(24); nws_t = alloc(8); sc_t = alloc(8); ssx = alloc(32); rsx = alloc(32)
        gate_bc = alloc(1024)
        B_const = Buf("const"); B_small = Buf("small"); B_mod = Buf("mod"); B_gate = Buf("gatebc")
        B_ssx = Buf("ssx")
        hT_off = cur[0]
        hT = alloc(8 * SEQ, BF16)
        wadaB = big[:, hT_off:hT_off + 8192]
        cosT = alloc(SEQ); sinT = alloc(SEQ)
        B_hT = [Buf("hT%d" % i) for i in range(8)]
        B_tab = Buf("tables")
        B_wbf = [Buf("wbf%d" % i) for i in range(20)]
        for b_ in B_wbf:
            b_.nobar = True
        B_hTd = Buf("hTd"); B_ofd = Buf("ofd")
        base_cur = cur[0]

        cast_secs = list(range(0, 6)) + list(range(16, 20))

        def emit_cast(sidx):
            dma("pool", wbf_d[:, sidx * 512:(sidx + 1) * 512], win_d[:, sidx * 512:(sidx + 1) * 512], B_wbf[sidx], writes=[B_wbf[sidx]])

        S.op("pool", lambda h: h.memset(ident_f, 0.0), writes=[B_const])
        S.op("pool", lambda h: h.affine_select(out=ident_f, in_=ident_f, compare_op=ALU.not_equal, fill=1.0, base=0, pattern=[[-1, 128]], channel_multiplier=1), reads=[B_const], writes=[B_const])
        S.op("pool", lambda h: h.memset(ones_f, 1.0), reads=[B_const], pwrites=[B_const])
        S.op("pool", lambda h: h.memset(ones_b, 1.0), reads=[B_const], pwrites=[B_const])
        cp("pool", ident_b, ident_f, reads=[B_const], pwrites=[B_const])
        S.op("pool", lambda h: h.memset(mask2, 1.0), reads=[B_const], pwrites=[B_const])
        S.op("pool", lambda h: h.affine_select(out=mask2[:, 0:128], in_=mask2[:, 0:128], compare_op=ALU.is_ge, fill=0.0, base=0, pattern=[[1, 128]], channel_multiplier=-1), reads=[B_const], writes=[B_const])
        S.op("pool", lambda h: h.affine_select(out=mask2[:, 128:256], in_=mask2[:, 128:256], compare_op=ALU.is_ge, fill=0.0, base=0, pattern=[[-1, 128]], channel_multiplier=1), reads=[B_const], writes=[B_const])
        ts("pool", maskneg, mask2, -1.0, 30000.0, ALU.add, ALU.mult, reads=[B_const], pwrites=[B_const])
        S.op("pool", lambda h: h.memset(sgn_t[0:64, :], -1.0), reads=[B_const], pwrites=[B_const])
        S.op("pool", lambda h: h.memset(sgn_t[64:128, :], 1.0), reads=[B_const], pwrites=[B_const])
        S.op("pool", lambda h: h.memset(ssx, 0.0), writes=[B_ssx])
        S.op("pool", lambda h: h.memset(eps_t, EPS), reads=[B_const], pwrites=[B_const])
        EPSB = eps_t[:, 0:1]

        B_small.newgen()
        for (t, src) in [(qw_t, qw_d), (kw_t, kw_d), (invf_t, invf_d)]:
            dma("sp", t, src.rearrange("(p o) -> p o", o=1), B_small, pwrites=[B_small], slow=True)
        if stop_after == "P0":
            S.final_wait("sp")
            S.emit()
            return nc
        stg = alloc(128); cwj = alloc(1024)
        B_stg = Buf("stg"); B_cwj = Buf("cwj")
        B_stg.newgen()
        r0 = 0
        for (src, nk) in [(c_d, 8), (normw_d, 8), (bada_d, 24), (convb_d, 8), (lnw_d, 8), (lnb_d, 8)]:
            dma("sp", stg[r0:r0 + nk, :], src.rearrange("(k p) -> k p", p=128), B_stg, pwrites=[B_stg])
            r0 += nk
        dma("sp", cwj[0:31, :], convw_d, B_cwj, writes=[B_cwj])
        S.op("pe", lambda h: h.transpose(out=pb[3][:, 0:64], in_=stg[0:64, :], identity=ident_f[0:64, 0:64]), reads=[B_stg, B_const], writes=[PB[3]])
        cp("dve", vecs, pb[3][:, 0:64], reads=[PB[3]], pwrites=[B_small])

        def cw_tr(h):
            for k in range(8):
                ins = h.transpose(out=pb[2][:, k * 32:k * 32 + 31], in_=cwj[0:31, k * 128:(k + 1) * 128], identity=ident_f[0:31, 0:31])
            return ins
        S.op("pe", cw_tr, reads=[B_cwj, B_const], writes=[PB[2]])
        cp("dve", VW(convw_t, [[31, 8], [1, 31]]), VW(pb[2], [[32, 8], [1, 31]]), reads=[PB[2]], pwrites=[B_small])

        wada = alloc(8 * 1024)
        B_wada = Buf("wada")
        e_t = alloc(8)
        B_e = Buf("e_t")
        act(e_t, c_t, AF.Exp, reads=[B_small], writes=[B_e], scale=-1.0)
        ts("dve", e_t, e_t, 1.0, None, ALU.add, None, reads=[B_e], writes=[B_e])
        S.op("dve", lambda h: h.reciprocal(out=e_t, in_=e_t), reads=[B_e], writes=[B_e])
        tt("dve", sc_t, c_t, e_t, ALU.mult, reads=[B_e, B_small], writes=[B_mod])
        dg_f = alloc(128)
        B_dg = Buf("dg_f")

        B_wadaB = Buf("wadaB")

        def mod_load(jh, buf, BUF):
            BUF.newgen()
            for k in range(8):
                dma("pool", buf[:, k * 1024:(k + 1) * 1024], wada_d[k * 128:(k + 1) * 128, jh * 1024:(jh + 1) * 1024], BUF, pwrites=[BUF])

        def mod_pass(jh, buf, BUF):
            mp, MPB = (pb[0], PB[0]) if jh == 0 else (pb[3], PB[3])

            def mod_mm(h, jh=jh):
                for jj in range(8):
                    for k in range(8):
                        ins = h.matmul(mp[:, jj:jj + 1], lhsT=buf[:, k * 1024 + jj * 128:k * 1024 + (jj + 1) * 128], rhs=sc_t[:, k:k + 1], start=(k == 0), stop=(k == 7))
                return ins
            S.op("pe", mod_mm, reads=[BUF, B_mod], writes=[MPB])
            tt("dve", mod_t[:, jh * 8:(jh + 1) * 8], mp[:, 0:8], bada_t[:, jh * 8:(jh + 1) * 8], ALU.add, reads=[MPB, B_small], pwrites=[B_mod])

        grow = alloc(1024); brow = alloc(1024)
        B_grow = Buf("grow")

        def gate_part():
            dma("sp", brow[0:1, :], bada_d[2048:3072].rearrange("(o n) -> o n", o=1), B_grow, writes=[B_grow])
            for hh in range(2):
                p_, P_ = pb[1 + hh], PB[1 + hh]
                mms(p_[0:1, :], [(sc_t[:, k:k + 1], wada[:, k * 1024 + hh * 512:k * 1024 + (hh + 1) * 512]) for k in range(8)],
                    reads=[B_wada, B_mod], writes=[P_])
                tt("dve", grow[0:1, hh * 512:(hh + 1) * 512], p_[0:1, :], brow[0:1, hh * 512:(hh + 1) * 512], ALU.add, reads=[P_, B_grow], pwrites=[B_grow])
            for hh in range(2):
                p_, P_ = pb[1 + hh], PB[1 + hh]
                mms(p_[:, :], [(ones_f[0:1, :], grow[0:1, hh * 512:(hh + 1) * 512])], reads=[B_grow, B_const], writes=[P_])
                ts("dve", gate_bc[:, hh * 512:(hh + 1) * 512], p_[:, :], 0.5, None, ALU.mult, None, reads=[P_], pwrites=[B_gate])

        if stop_after == "P1":
            S.final_wait("sp")
            S.emit()
            return nc
        INV2PI = float(1.0 / (2 * np.pi)); C1 = 6.28125; C2 = float(2 * np.pi - 6.28125)
        PI = float(np.pi); TWO_PI = float(2 * np.pi)
        B_tab.newgen()

        def mk_tabset(n, tag, share=None):
            d_ = dict(n=n, posi=(alloc(n) if share is None else None), ang=alloc(n), yy=alloc(n))
            if share is None:
                d_["kf"] = alloc(n); d_["mm"] = alloc(n)
                d_["B"] = [Buf(tag + x) for x in ("posi", "ang", "kf", "yy", "mm")]
            else:
                d_["kf"] = share["kf"]; d_["mm"] = share["mm"]
                d_["B"] = [Buf(tag + "posi"), Buf(tag + "ang"), share["B"][2], Buf(tag + "yy"), share["B"][4]]
            return d_
        tabD = mk_tabset(1024, "tD")

        def table_chunk(eng, tb_, c0):
            n = tb_["n"]
            posi, ang, kf, yy, mm_ = tb_["posi"], tb_["ang"], tb_["kf"], tb_["yy"], tb_["mm"]
            B_posi, B_ang, B_kf, B_yy, B_mm = tb_["B"]
            sl = slice(c0, c0 + n)

            def fma(out, a, sc, b, reads, OUT):
                if eng == "dve":
                    stt("dve", out, a, sc, b, ALU.mult, ALU.add, reads=reads, writes=[OUT])
                else:
                    ts(eng, mm_, a, sc, None, ALU.mult, None, reads=reads, writes=[B_mm])
                    tt(eng, out, mm_, b, ALU.add, reads=reads + [B_mm], writes=[OUT])
            if eng == "dve":
                dma("pool", posi.bitcast(I32), pos_d[sl].partition_broadcast(128), B_posi, writes=[B_posi])
                cp(eng, ang, posi.bitcast(I32), reads=[B_posi], writes=[B_ang])
            else:
                cp(eng, ang, posP[:, c0 - 3072:c0 - 3072 + n].bitcast(I32), reads=[B_posP], writes=[B_ang])
            ts(eng, ang, ang, invf_t[:, 0:1], None, ALU.mult, None, reads=[B_ang, B_small], writes=[B_ang])
            ts(eng, kf.bitcast(I32), ang, INV2PI, None, ALU.mult, None, reads=[B_ang], writes=[B_kf])
            cp(eng, kf, kf.bitcast(I32), reads=[B_kf], writes=[B_kf])
            fma(yy, kf, -C1, ang, [B_kf, B_ang], B_yy)
            fma(yy, kf, -C2, yy, [B_kf, B_yy], B_yy)
            ts(eng, kf, yy, PI, None, ALU.is_gt, None, reads=[B_yy], writes=[B_kf])
            fma(yy, kf, -TWO_PI, yy, [B_kf, B_yy], B_yy)
            ts(eng, kf, yy, -PI, None, ALU.is_lt, None, reads=[B_yy], writes=[B_kf])
            fma(yy, kf, TWO_PI, yy, [B_kf, B_yy], B_yy)
            ts(eng, ang, yy, PI / 2, None, ALU.add, None, reads=[B_yy], writes=[B_ang])
            ts(eng, kf, ang, PI, None, ALU.is_gt, None, reads=[B_ang], writes=[B_kf])
            fma(ang, kf, -TWO_PI, ang, [B_kf, B_ang], B_ang)

            def act_part():
                act(sinT[:, sl], yy, AF.Sin, reads=[B_yy, B_const], pwrites=[B_tab], scale=sgn_t[:, 0:1])
                act(cosT[:, sl], ang, AF.Sin, reads=[B_ang], pwrites=[B_tab])
            if eng == "dve":
                act_part()
                return None
            return act_part

        NXB = 5
        xt = [alloc(1024) for _ in range(NXB)]
        xn = [alloc(1024) for _ in range(2)]
        junk = alloc(1024, BF16)
        lnt = alloc(1)
        B_xt = [Buf("xt%d" % i) for i in range(NXB)]
        B_xn = [Buf("xn%d" % i) for i in range(2)]
        B_junk = Buf("junk"); B_lnt = Buf("lnt"); B_rsx = Buf("rsx")
        import os as _os
        NT = int(_os.environ.get('H_TILES', SEQ // 128))
        lnx = alloc(32)
        mod_load(0, wada, B_wada)
        mod_load(1, wadaB, B_wadaB)
        for i in range(NT):
            xb, XB = xt[i % NXB], B_xt[i % NXB]
            dma("sp", xb, x_d[i * 128:(i + 1) * 128, :], XB, writes=[XB])
            act(junk, xb, AF.Square, reads=[XB, B_ssx], writes=[B_junk], accum=ssx[:, i:i + 1])
            if i % 8 == 3:
                table_chunk("dve", tabD, (i // 8) * 1024)
        act(lnx, ssx, AF.Ln, reads=[B_junk, B_ssx], writes=[B_lnt], scale=1.0 / D, bias=EPSB)
        act(rsx, lnx, AF.Exp, reads=[B_lnt], writes=[B_rsx], scale=-0.5)
        mod_pass(0, wada, B_wada)
        mod_load(2, wada, B_wada)
        mod_pass(1, wadaB, B_wadaB)
        stt("dve", nws_t, mod_t[:, 8:16], 1.0, normw_t, ALU.add, ALU.mult, reads=[B_mod, B_small], pwrites=[B_mod])

        def h_scale(i):
            xb, XB = xt[(NT + i) % NXB], B_xt[(NT + i) % NXB]
            dma("sp", xb, x_d[i * 128:(i + 1) * 128, :], XB, writes=[XB])
            nb_, NB_ = xn[i % 2], B_xn[i % 2]
            act(nb_, xb, AF.Copy, reads=[XB, B_rsx], writes=[NB_], scale=rsx[:, i:i + 1])

        def h_rest(i):
            T = i // 4
            if i % 4 == 0:
                B_hT[T].newgen()
            nb_, NB_ = xn[i % 2], B_xn[i % 2]
            banks = []
            for half in range(2):
                p_, P_ = bank(0, 4)

                def trs(h, half=half, p_=p_, nb_=nb_):
                    for kk in range(4):
                        k = half * 4 + kk
                        ins = h.transpose(out=p_[:, kk * 128:(kk + 1) * 128], in_=nb_[:, k * 128:(k + 1) * 128], identity=ident_f)
                    return ins
                S.op("pe", trs, reads=[NB_, B_const], writes=[P_])
                banks.append((p_, P_))
            for half in range(2):
                p_, P_ = banks[half]
                for kk in range(4):
                    k = half * 4 + kk
                    dst = hT[:, k * SEQ + i * 128:k * SEQ + (i + 1) * 128]
                    if half == 0:
                        act(dst, p_[:, kk * 128:(kk + 1) * 128], AF.Identity, reads=[P_, B_mod], pwrites=[B_hT[T]], scale=nws_t[:, k:k + 1], bias=mod_t[:, k:k + 1])
                    else:
                        ts("dve", dst, p_[:, kk * 128:(kk + 1) * 128], nws_t[:, k:k + 1], mod_t[:, k:k + 1], ALU.mult, ALU.add, reads=[P_, B_mod], pwrites=[B_hT[T]])
            if i % 4 == 3:
                dma("pool", hT_d[:, :, T * 512:(T + 1) * 512], VW(hT, [[SEQ, 8], [1, 512]], off=T * 512), B_hTd, reads=[B_hT[T]], pwrites=[B_hTd])
        if NT > 0:
            h_scale(0)
        for i in range(NT):
            if i + 1 < NT:
                h_scale(i + 1)
            h_rest(i)
        gate_part()
        S.barrier()
        cur[0] = base_cur
        if stop_after == "H":
            S.final_wait("sp")
            S.emit()
            return nc

        accU = alloc(SEQ); accZ = alloc(SEQ)
        QT = alloc(SEQ, BF16); KT = alloc(SEQ, BF16); Vh = alloc(32 * 128, BF16)
        wq = [alloc(1024, BF16) for _ in range(2)]
        wk = [alloc(1024, BF16) for _ in range(2)]
        wv = [alloc(1024, BF16) for _ in range(2)]
        wga = alloc(1024, BF16)
        sqb = [alloc(512, BF16) for _ in range(3)]
        NTMP = 9
        tmp = [alloc(512) for _ in range(NTMP)]
        B_tmp = [Buf("tmp%d" % i) for i in range(NTMP)]
        PT = [alloc(256, BF16) for _ in range(4)]
        oft = [alloc(512, BF16) for _ in range(2)]
        B_accU = Buf("accU"); B_accZ = Buf("accZ")
        B_QT = [Buf("QT%d" % i) for i in range(8)]; B_KT = [Buf("KT%d" % i) for i in range(8)]
        B_Vh = Buf("Vh")
        B_wq = [Buf("wq%d" % i) for i in range(2)]; B_wk = [Buf("wk%d" % i) for i in range(2)]; B_wv = [Buf("wv%d" % i) for i in range(2)]
        B_wga = Buf("wga")
        B_sqb = [Buf("sqb%d" % i) for i in range(3)]
        B_PT = [Buf("PT%d" % i) for i in range(4)]
        B_oft = [Buf("oft%d" % i) for i in range(2)]
        tr = [0]

        def gtmp():
            i = tr[0] % NTMP
            tr[0] += 1
            return tmp[i], B_tmp[i]

        def wload(dst, DST, col0):
            dma("pool", VW(dst, [[128, 8], [1, 128]]), win_d[:, col0:col0 + 128].rearrange("(k p) c -> p k c", p=128), DST, writes=[DST])

        heads = [(h, g) for h in range(4) for g in range(3)]
        def load_head_w(idx):
            h, g = heads[idx]
            hd = 4 * g + h
            s_ = idx % 2
            wload(wq[s_], B_wq[s_], 3072 + hd * 128)
            wload(wk[s_], B_wk[s_], 4608 + hd * 128)
            wload(wv[s_], B_wv[s_], 6144 + hd * 128)
        load_head_w(0)
        SC = float(128 ** -0.5)
        item = [0]
        for idx, (h, g) in enumerate(heads):
            win_, dil = GROUPS[g]
            span = dil * 128
            nb = SEQ // span
            ws = idx % 2
            if idx + 1 < len(heads):
                load_head_w(idx + 1)
            if idx < len(cast_secs):
                emit_cast(cast_secs[idx])
            if g == 0:
                wload(wga, B_wga, 7680 + h * 128)
            items = [(T, which) for T in range(8) for which in range(2)]
            st_ = {}

            def p2_a(j):
                T, which = items[j]
                wmat, WB = (wq[ws], B_wq[ws]) if which == 0 else (wk[ws], B_wk[ws])
                p_, P_ = bank(0, 8)
                mms(p_[:, :], [(wmat[:, k * 128:(k + 1) * 128], hT[:, k * SEQ + T * 512:k * SEQ + (T + 1) * 512]) for k in range(8)],
                    reads=[WB, B_hT[T]], writes=[P_])
                sb_, SB_ = sqb[item[0] % 3], B_sqb[item[0] % 3]
                item[0] += 1
                act(sb_, p_[:, :], AF.Square, reads=[P_], writes=[SB_])
                st_[j] = dict(p_=p_, P_=P_, sb_=sb_, SB_=SB_)

            def p2_a2(j):
                d_ = st_[j]
                p2, P2 = bank(0, 8)
                mms(p2[:, :], [(ones_b, d_["sb_"])], reads=[d_["SB_"], B_const], writes=[P2])
                d_.update(p2=p2, P2=P2)

            def p2_b(j):
                T, which = items[j]
                d_ = st_[j]
                wvec = qw_t if which == 0 else kw_t
                rs_, RS_ = gtmp()
                act(rs_, d_["p2"][:, :], AF.Ln, reads=[d_["P2"]], writes=[RS_], scale=1.0 / 128, bias=EPSB)
                act(rs_, rs_, AF.Exp, reads=[RS_], writes=[RS_], scale=-0.5)
                qn, QN = gtmp()
                stt("dve", qn, d_["p_"][:, :], wvec[:, 0:1], rs_, ALU.mult, ALU.mult, reads=[d_["P_"], RS_, B_small], writes=[QN])
                d_.update(rs_=rs_, RS_=RS_, qn=qn, QN=QN)

            def p2_c(j):
                T, which = items[j]
                d_ = st_.pop(j)
                tsl = slice(T * 512, (T + 1) * 512)
                dstT, DB = (QT, B_QT[T]) if which == 0 else (KT, B_KT[T])
                rs_, RS_, qn, QN = d_["rs_"], d_["RS_"], d_["qn"], d_["QN"]
                qs, QS = gtmp()
                QS.newgen()
                act(qs[0:64, :], qn[64:128, :], AF.Copy, reads=[QN], pwrites=[QS])
                act(qs[64:128, :], qn[0:64, :], AF.Copy, reads=[QN], pwrites=[QS])
                tt("dve", rs_, qn, cosT[:, tsl], ALU.mult, reads=[QN, B_tab], writes=[RS_])
                tt("pool", qs, qs, sinT[:, tsl], ALU.mult, reads=[QS, B_tab], writes=[QS])
                tt("pool", dstT[:, tsl], rs_, qs, ALU.add, reads=[RS_, QS], writes=[DB])

            nit = len(items)
            for j in range(nit + 3):
                if j < nit:
                    p2_a(j)
                if 0 <= j - 1 < nit:
                    p2_a2(j - 1)
                if 0 <= j - 2 < nit:
                    p2_b(j - 2)
                if 0 <= j - 3 < nit:
                    p2_c(j - 3)
            blocks = [(r, n) for r in range(dil) for n in range(nb)]
            B_Vh.newgen()
            for b0 in range(0, 32, 4):
                p_, P_ = bank(0, 4)
                P_.newgen()
                for s_ in range(4):
                    r, n = blocks[b0 + s_]
                    Tt = (n * span) // 512
                    rd = [B_hT[t_] for t_ in range((n * span) // 512, min(8, ((n + 1) * span + 511) // 512))]
                    mms(p_[:, s_ * 128:(s_ + 1) * 128],
                        [(VW(hT, [[dil, 128]], off=k * SEQ + n * span + r), wv[ws][:, k * 128:(k + 1) * 128]) for k in range(8)],
                        reads=[B_wv[ws]] + rd, pwrites=[P_])
                if (b0 // 4) % 2 == 0:
                    act(Vh[:, b0 * 128:(b0 + 4) * 128], p_[:, :], AF.Copy, reads=[P_], pwrites=[B_Vh])
                else:
                    cp("dve", Vh[:, b0 * 128:(b0 + 4) * 128], p_[:, :], reads=[P_], pwrites=[B_Vh])
            nbat = 4 if nb >= 4 else nb
            pts = {}

            def p4_s(bi):
                r, n = blocks[bi]
                last = (n == nb - 1)
                nq = 1 if last else 2
                kT_tiles = [B_KT[t_] for t_ in range((n * span) // 512, min(8, ((n + 1) * span + 511) // 512))]
                qT_tiles = [B_QT[t_] for t_ in range((n * span) // 512, min(8, ((n + nq) * span + 511) // 512))]
                p_, P_ = bank(0, 4)
                mms(p_[:, 0:nq * 128], [(VW(KT, [[dil, 128]], off=n * span + r), VW(QT, [[span, nq], [dil, 128]], off=n * span + r)),
                                         (ident_b, maskneg[:, 0:nq * 128])],
                    reads=kT_tiles + qT_tiles + [B_const], writes=[P_])
                pt_, PT_ = PT[bi % 4], B_PT[bi % 4]
                act(pt_[:, 0:nq * 128], p_[:, 0:nq * 128], AF.Exp, reads=[P_], writes=[PT_], scale=SC)
                pts[bi] = (pt_, PT_)

            def p4_pv(bi):
                r, n = blocks[bi]
                pt_, PT_ = pts[bi]
                s_ = n % nbat
                if s_ == 0:
                    pts["u"] = (pb[4 + (bi // nbat) % 2], PB[4 + (bi // nbat) % 2], pb[6 + (bi // nbat) % 2], PB[6 + (bi // nbat) % 2])
                    pts["u"][1].newgen(); pts["u"][3].newgen()
                up, UP, zp, ZP = pts["u"]
                pairs_u = []
                pairs_z = []
                rds = [PT_, B_Vh, B_const]
                if n > 0:
                    ppt, PPT = pts[bi - 1]
                    pairs_u.append((Vh[:, (bi - 1) * 128:bi * 128], ppt[:, 128:256]))
                    pairs_z.append((ones_b, ppt[:, 128:256]))
                    rds.append(PPT)
                pairs_u.append((Vh[:, bi * 128:(bi + 1) * 128], pt_[:, 0:128]))
                pairs_z.append((ones_b, pt_[:, 0:128]))
                mms(up[:, s_ * 128:(s_ + 1) * 128], pairs_u, reads=rds, pwrites=[UP])
                mms(zp[:, s_ * 128:(s_ + 1) * 128], pairs_z, reads=rds, pwrites=[ZP])
                if bi - 1 in pts and bi >= 1:
                    pass
                if s_ == nbat - 1:
                    n0 = n - (nbat - 1)
                    vdims = [[span, nbat], [dil, 128]]
                    voff = n0 * span + r
                    au = VW(accU, vdims, off=voff)
                    az = VW(accZ, vdims, off=voff)
                    pu = VW(up, [[128, nbat], [1, 128]])
                    pz = VW(zp, [[128, nbat], [1, 128]])
                    if g == 0:
                        act(au, pu, AF.Copy, reads=[UP], pwrites=[B_accU])
                        cp("dve", az, pz, reads=[ZP], pwrites=[B_accZ])
                    else:
                        tt("dve", au, pu, au, ALU.add, reads=[UP, B_accU], pwrites=[B_accU])
                        tt("dve", az, pz, az, ALU.add, reads=[ZP, B_accZ], pwrites=[B_accZ])

            nblk = len(blocks)
            for bi in range(nblk + 1):
                if bi < nblk:
                    p4_s(bi)
                if bi >= 1:
                    p4_pv(bi - 1)
                    pts.pop(bi - 3, None)
            if g == 2:
                fin = {}

                def fin_a(T):
                    tsl = slice(T * 512, (T + 1) * 512)
                    p_, P_ = bank(0, 4)
                    mms(p_[:, :], [(wga[:, k * 128:(k + 1) * 128], hT[:, k * SEQ + T * 512:k * SEQ + (T + 1) * 512]) for k in range(8)],
                        reads=[B_wga, B_hT[T]], writes=[P_])
                    e_, E_ = gtmp()
                    act(e_, p_[:, :], AF.Exp, reads=[P_], writes=[E_], scale=-1.0)
                    num, NUM = gtmp()
                    tt("dve", num, p_[:, :], accU[:, tsl], ALU.mult, reads=[P_, B_accU], writes=[NUM])
                    stt("dve", e_, e_, 1.0, accZ[:, tsl], ALU.add, ALU.mult, reads=[E_, B_accZ], writes=[E_])
                    fin[T] = (e_, E_, num, NUM)

                def fin_b(T):
                    tsl = slice(T * 512, (T + 1) * 512)
                    e_, E_, num, NUM = fin.pop(T)
                    act(e_, e_, AF.Ln, reads=[E_], writes=[E_])
                    act(e_, e_, AF.Exp, reads=[E_], writes=[E_], scale=-1.0)
                    ob, OB = oft[T % 2], B_oft[T % 2]
                    tt("pool", ob, num, e_, ALU.mult, reads=[NUM, E_], writes=[OB])
                    dma("pool", of_d[h][:, tsl], ob, OB, reads=[OB], pwrites=[B_ofd])
                for T in range(9):
                    if T < 8:
                        fin_a(T)
                    if T >= 1:
                        fin_b(T - 1)
                B_accU.newgen(); B_accZ.newgen()
        S.barrier()
        cur[0] = base_cur
        if stop_after == "1":
            S.final_wait("sp")
            S.emit()
            return nc

        cur[0] = base_cur - (8 * SEQ // 2 + 2 * SEQ)
        wco = alloc(8 * 1024, BF16); wao = alloc(4 * 1024, BF16); wout = alloc(8 * 1024, BF16)
        B_wres = Buf("wres")
        B_wres.newgen()

        def load_wres(after):
            dma("pool", VW(wco, [[1024, 8], [1, 1024]]), wco_d.rearrange("(k p) n -> p k n", p=128), B_wres, reads=after, pwrites=[B_wres])
            dma("pool", VW(wao, [[1024, 4], [1, 1024]]), wao_d.rearrange("(k p) n -> p k n", p=128), B_wres, reads=after, pwrites=[B_wres])
            dma("pool", VW(wout, [[1024, 8], [1, 1024]]), wout_d.rearrange("(k p) n -> p k n", p=128), B_wres, reads=after, pwrites=[B_wres])
        NRING = 8
        ring = [alloc(1024, BF16) for _ in range(NRING)]
        B_ring = [Buf("ring%d" % i) for i in range(NRING)]
        hTt = [alloc(8 * 512, BF16) for _ in range(2)]
        B_hTt = [Buf("hTt%d" % i) for i in range(2)]
        oft2 = [alloc(4 * 512, BF16) for _ in range(2)]
        B_oft2 = [Buf("oft2_%d" % i) for i in range(2)]
        upad = [alloc(544, BF16) for _ in range(2)]
        B_upad = [Buf("upad%d" % i) for i in range(2)]
        halo = alloc(8 * 32, BF16)
        B_halo = [Buf("halo%d" % i) for i in range(8)]
        dgt = [alloc(31 * 128, BF16) for _ in range(3)]
        B_dgt = [Buf("dgt%d" % i) for i in range(3)]
        uc = alloc(8 * 512)
        B_uc = [Buf("uc%d" % i) for i in range(8)]
        NTB = 4
        tb = [alloc(512, BF16) for _ in range(NTB)]
        B_tb = [Buf("tb%d" % i) for i in range(NTB)]
        NT2 = 12
        tmp2 = [alloc(512) for _ in range(NT2)]
        B_tmp2 = [Buf("tmp2_%d" % i) for i in range(NT2)]
        stat = [alloc(512) for _ in range(3)]
        B_stat = [Buf("stat%d" % i) for i in range(3)]
        ufin = alloc(8 * 512, BF16)
        B_ufin = [Buf("ufin%d" % i) for i in range(8)]
        yb = alloc(8 * 512, BF16)
        B_yb = [Buf("yb%d" % i) for i in range(8)]
        xres = [alloc(1024) for _ in range(2)]
        B_xres = [Buf("xres%d" % i) for i in range(2)]
        res = [alloc(1024) for _ in range(2)]
        B_res = [Buf("res%d" % i) for i in range(2)]
        t2c = [0]
        tbc = [0]

        def gt2():
            i = t2c[0] % NT2
            t2c[0] += 1
            return tmp2[i], B_tmp2[i]

        def gtb():
            i = tbc[0] % NTB
            tbc[0] += 1
            return tb[i], B_tb[i]

        def col_of(ci):
            sec, c = ci // 8, ci % 8
            base = [0, 1024, 2048, 8192, 9216][sec]
            return base + c * 128
        order = []
        for c in range(8):
            order += [0 * 8 + c, 1 * 8 + c]
        order += [2 * 8 + c for c in range(8)]
        for f in range(8):
            order += [3 * 8 + f, 4 * 8 + f]
        stream = [(t, ci) for t in range(8) for ci in order]
        issued = [0]
        used = [0]

        def wnext():
            while issued[0] < len(stream) and issued[0] < used[0] + NRING:
                j = issued[0]
                t_, ci = stream[j]
                col0 = col_of(ci)
                dma("sp", VW(ring[j % NRING], [[128, 8], [1, 128]]), wbf_d[:, col0:col0 + 128].rearrange("(k p) c -> p k c", p=128),
                    B_ring[j % NRING], reads=[B_wbf[col0 // 512]], writes=[B_ring[j % NRING]])
                issued[0] += 1
            j = used[0]
            used[0] += 1
            return ring[j % NRING], B_ring[j % NRING], stream[j][1]

        def proj(p_, P_, hb, HB):
            w_, W_, ci = wnext()
            mms(p_[:, :], [(w_[:, k * 128:(k + 1) * 128], hb[:, k * 512:(k + 1) * 512]) for k in range(8)], reads=[W_, HB], writes=[P_])
            return ci

        s1p, S1P = pb[6], PB[6]
        s2p, S2P = pb[7], PB[7]

        def tile_loads(t_):
            sl_ = slice(t_ * 512, (t_ + 1) * 512)
            dma("sp", VW(hTt[t_ % 2], [[512, 8], [1, 512]]), hT_d[:, :, sl_], B_hTt[t_ % 2], reads=[B_hTd], writes=[B_hTt[t_ % 2]])
            dma("sp", VW(oft2[t_ % 2], [[512, 4], [1, 512]]), of_d[:, :, sl_].rearrange("s p t -> p s t"), B_oft2[t_ % 2], reads=[B_ofd], writes=[B_oft2[t_ % 2]])
        for t in range(8):
            tsl = slice(t * 512, (t + 1) * 512)
            hb, HB = hTt[t % 2], B_hTt[t % 2]
            ofb, OFB = oft2[t % 2], B_oft2[t % 2]
            if t == 0:
                tile_loads(0)
            S1P.newgen(); S2P.newgen()
            pend_st = []
            def do_diag(gi):
                c = gi % 8
                db, DB_ = dgt[gi % 3], B_dgt[gi % 3]
                tt("dve", VW(db, [[128, 31], [1, 128]]), VW(ident_b, [[0, 31], [1, 128]]), VW(convw_t, [[1, 31], [0, 128]], off=c * 31), ALU.mult,
                   reads=[B_const, B_small], writes=[DB_])

            def do_ab(t_, c):
                hb_, HB_ = hTt[t_ % 2], B_hTt[t_ % 2]
                ub, UB = upad[c % 2], B_upad[c % 2]
                pa, PA = bank(0, 6)
                ci = proj(pa, PA, hb_, HB_)
                assert ci == c
                pbb, PBB = bank(0, 6)
                ci = proj(pbb, PBB, hb_, HB_)
                assert ci == 8 + c
                th, TH = gt2()
                act(th, pbb[:, :], AF.Tanh, reads=[PBB], writes=[TH], scale=0.5)
                UB.newgen()
                if t_ == 0:
                    S.op("pool", (lambda ub: lambda h: h.memset(ub[:, 0:30], 0.0))(ub), pwrites=[UB])
                else:
                    cp("pool", ub[:, 0:30], halo[:, c * 32:c * 32 + 30], reads=[B_halo[c]], pwrites=[UB])
                stt("dve", ub[:, 30:542], th, 1.0, pa[:, :], ALU.add, ALU.mult, reads=[TH, PA], pwrites=[UB])
                cp("pool", halo[:, c * 32:c * 32 + 30], ub[:, 512:542], reads=[UB], writes=[B_halo[c]])

            def do_conv(c):
                gi = t * 8 + c
                ub, UB = upad[c % 2], B_upad[c % 2]
                db, DB_ = dgt[gi % 3], B_dgt[gi % 3]
                pc, PC = bank(0, 6)
                mms(pc[:, :], [(db[:, j * 128:(j + 1) * 128], ub[:, j:j + 512]) for j in range(31)], reads=[UB, DB_], writes=[PC])
                act(uc[:, c * 512:(c + 1) * 512], pc[:, :], AF.Identity, reads=[PC, B_small], writes=[B_uc[c]], scale=0.5, bias=convb_t[:, c:c + 1])
                sq_, SQ_ = gtb()
                act(sq_, pc[:, :], AF.Square, reads=[PC, B_small], writes=[SQ_], scale=0.5, bias=convb_t[:, c:c + 1])
                ucb, UCB = gtb()
                cp("pool", ucb, uc[:, c * 512:(c + 1) * 512], reads=[B_uc[c]], writes=[UCB])

                def st_mm(h, c=c, ucb=ucb, sq_=sq_):
                    h.matmul(s1p[:, :], lhsT=ones_b, rhs=ucb, start=(c == 0), stop=(c == 7))
                    return h.matmul(s2p[:, :], lhsT=ones_b, rhs=sq_, start=(c == 0), stop=(c == 7))
                pend_st.append(lambda: S.op("pe", st_mm, reads=[UCB, SQ_, B_const], pwrites=[S1P, S2P]))

            if t == 0:
                do_diag(0)
                do_diag(1)
                do_ab(0, 0)
                load_wres([B_upad[0]])
            for c in range(8):
                if c + 1 < 8:
                    do_ab(t, c + 1)
                if t * 8 + c + 2 < 64:
                    do_diag(t * 8 + c + 2)
                do_conv(c)
                if len(pend_st) > 1:
                    pend_st.pop(0)()
            while pend_st:
                pend_st.pop(0)()
            mean, MEAN = stat[0], B_stat[0]
            rstd, RSTD = stat[1], B_stat[1]
            mr, MR = stat[2], B_stat[2]
            ts("dve", mean, s1p[:, :], 1.0 / D, None, ALU.mult, None, reads=[S1P], writes=[MEAN])
            tt("pool", mr, mean, mean, ALU.mult, reads=[MEAN], writes=[MR])
            stt("dve", rstd, s2p[:, :], 1.0 / D, mr, ALU.mult, ALU.subtract, reads=[S2P, MR], writes=[RSTD])
            act(rstd, rstd, AF.Ln, reads=[RSTD], writes=[RSTD], bias=EPSB)
            act(rstd, rstd, AF.Exp, reads=[RSTD], writes=[RSTD], scale=-0.5)
            tt("pool", mr, mean, rstd, ALU.mult, reads=[MEAN, RSTD], writes=[MR])
            for c in range(8):
                pg, PG = bank(0, 6)
                ci = proj(pg, PG, hb, HB)
                assert ci == 16 + c
                sg, SG = gt2()
                act(sg, pg[:, :], AF.Silu, reads=[PG], writes=[SG])
                t_, T_ = gt2()
                tt("dve", t_, uc[:, c * 512:(c + 1) * 512], rstd, ALU.mult, reads=[B_uc[c], RSTD], writes=[T_])
                tt("dve", t_, t_, mr, ALU.subtract, reads=[T_, MR], writes=[T_])
                act(t_, t_, AF.Silu, reads=[T_, B_small], writes=[T_], scale=lnw_t[:, c:c + 1], bias=lnb_t[:, c:c + 1])
                tt("dve", ufin[:, c * 512:(c + 1) * 512], t_, sg, ALU.mult, reads=[T_, SG], writes=[B_ufin[c]])
            if t + 1 < 8:
                tile_loads(t + 1)
            pend_yc = []
            for f in range(8):
                pm, PM = bank(0, 8)
                ci = proj(pm, PM, hb, HB)
                assert ci == 24 + f
                tmc, TMC = gt2()
                act(tmc, pm[:, :], AF.Tanh, reads=[PM], writes=[TMC], scale=0.5)
                pm2, PM2 = bank(0, 8)
                ci = proj(pm2, PM2, hb, HB)
                assert ci == 32 + f
                tma, TMA = gt2()
                act(tma, pm2[:, :], AF.Tanh, reads=[PM2], writes=[TMA], scale=0.5)
                pa_, PA_ = bank(0, 8)
                mms(pa_[:, :], [(wao[:, s_ * 1024 + f * 128:s_ * 1024 + (f + 1) * 128], ofb[:, s_ * 512:(s_ + 1) * 512]) for s_ in range(4)],
                    reads=[B_wres, OFB], writes=[PA_])
                stt("dve", tma, tma, 1.0, pa_[:, :], ALU.add, ALU.mult, reads=[TMA, PA_], writes=[TMA])

                def yc_part(f=f, tmc=tmc, TMC=TMC, tma=tma, TMA=TMA):
                    pc_, PC_ = bank(0, 8)
                    mms(pc_[:, :], [(wco[:, c * 1024 + f * 128:c * 1024 + (f + 1) * 128], ufin[:, c * 512:(c + 1) * 512]) for c in range(8)],
                        reads=[B_wres] + B_ufin, writes=[PC_])
                    stt("dve", tmc, tmc, 1.0, pc_[:, :], ALU.add, ALU.mult, reads=[TMC, PC_], writes=[TMC])
                    tt("pool", yb[:, f * 512:(f + 1) * 512], tmc, tma, ALU.add, reads=[TMC, TMA], writes=[B_yb[f]])
                pend_yc.append(yc_part)
                if len(pend_yc) > 1:
                    pend_yc.pop(0)()
            while pend_yc:
                pend_yc.pop(0)()
            if t + 1 < 8:
                do_ab(t + 1, 0)
            for s_ in range(4):
                row0 = t * 512 + s_ * 128
                xb, XB = xres[s_ % 2], B_xres[s_ % 2]
                dma("sp", xb, x_d[row0:row0 + 128, :], XB, writes=[XB])
                rb, RB = res[s_ % 2], B_res[s_ % 2]
                RB.newgen()
                for half in range(2):
                    po, PO = bank(0, 6)
                    mms(po[:, :], [(yb[:, f * 512 + s_ * 128:f * 512 + (s_ + 1) * 128], wout[:, f * 1024 + half * 512:f * 1024 + (half + 1) * 512]) for f in range(8)],
                        reads=[B_wres] + B_yb, writes=[PO])
                    hs = slice(half * 512, (half + 1) * 512)
                    tt("dve", rb[:, hs], po[:, :], gate_bc[:, hs], ALU.mult, reads=[PO, B_gate], pwrites=[RB])
                    tt("pool", rb[:, hs], rb[:, hs], xb[:, hs], ALU.add, reads=[RB, XB], pwrites=[RB])
                dma("pool", out_d[row0:row0 + 128, :], rb, RB, reads=[RB])
        S.final_wait("sp")
        S.emit()
        print("SBUF words hi", hi[0], "instr counts", {k: len(v.ops) for k, v in S.engs.items()})
    return nc


_NC_CACHE = {}


def _make_in_maps(x, c, positions, norm_w, w_ada, b_ada, w_in, conv_w, conv_b, conv_ln_w, conv_ln_b,
                  w_conv_out, q_norm_w, k_norm_w, w_attn_out, w_out):
    f = lambda a: np.ascontiguousarray(np.asarray(a, dtype=np.float32))
    inv = (10000.0 ** (-np.arange(0, 128, 2, dtype=np.float32) / 128)).astype(np.float32)
    invf = np.concatenate([inv, inv]).astype(np.float32)
    shared = {
        "norm_w": f(norm_w), "w_ada": f(w_ada), "b_ada": f(b_ada), "w_in": f(w_in), "conv_w": f(conv_w),
        "conv_b": f(conv_b), "conv_ln_w": f(conv_ln_w), "conv_ln_b": f(conv_ln_b), "w_conv_out": f(w_conv_out),
        "q_norm_w": f(q_norm_w), "k_norm_w": f(k_norm_w), "w_attn_out": f(w_attn_out), "w_out": f(w_out), "invf": invf,
    }
    x = np.asarray(x, dtype=np.float32)
    c = np.asarray(c, dtype=np.float32)
    positions = np.asarray(positions, dtype=np.int32)
    in_maps = []
    for b in range(NCORE):
        m = dict(shared)
        m["x"] = np.ascontiguousarray(x[b])
        m["c"] = np.ascontiguousarray(c[b])
        m["pos"] = np.ascontiguousarray(positions[b])
        in_maps.append(m)
    return in_maps


def kernel(**inputs):
    in_maps = _make_in_maps(**inputs)
    if "nc" not in _NC_CACHE:
        _NC_CACHE["nc"] = build_nc()
    nc = _NC_CACHE["nc"]
    res = run_bass_kernel_spmd(nc, in_maps, core_ids=list(range(NCORE)))
    out = np.stack([np.asarray(res.results[b]["out"], dtype=np.float32) for b in range(NCORE)], axis=0)
    return out
```
